# Optimizing a Trainium2 kernel written in Bass

```python
import math
import jax, jax.numpy as jnp
from jax import lax
import numpy as np

D_MODEL = 1024
BATCH = 8
SEQ = 2048
DEPTH = 2

BRANCH_WIDTH = D_MODEL // 2
N_BRANCHES = 4
CHUNK = 64
CONV_WIDTH = 4
NORM_EPS = 1e-6

HGRN_HEADS = 4
HGRN_EXPAND = BRANCH_WIDTH // HGRN_HEADS
HGRN_FDIM = HGRN_HEADS * HGRN_EXPAND
HGRN_VDIM = BRANCH_WIDTH
HGRN_VHEAD = HGRN_VDIM // HGRN_HEADS
SSD_DINNER = BRANCH_WIDTH
SSD_HEADDIM = 64
SSD_HEADS = SSD_DINNER // SSD_HEADDIM
SSD_GROUPS = 2
SSD_HPG = SSD_HEADS // SSD_GROUPS
SSD_STATE = 64
SSD_CONV_DIM = SSD_DINNER + 2 * SSD_GROUPS * SSD_STATE
GLA_HEADS = 4
GLA_KDIM = BRANCH_WIDTH // 2
GLA_VDIM = BRANCH_WIDTH
GLA_HEAD_K = GLA_KDIM // GLA_HEADS
GLA_HEAD_V = GLA_VDIM // GLA_HEADS
GLA_GATE_RANK = 16
GLA_GATE_NORMALIZER = 16.0
LRU_WIDTH = BRANCH_WIDTH
LRU_BLOCKS = 8
LRU_BLOCK = LRU_WIDTH // LRU_BLOCKS
LRU_C = 8.0
D_FF = ((8 * D_MODEL // 3 + 255) // 256) * 256

IN_SPLITS = (
    HGRN_FDIM, HGRN_FDIM, HGRN_VDIM, HGRN_VDIM,
    SSD_DINNER, SSD_CONV_DIM, SSD_HEADS,
    GLA_KDIM, GLA_KDIM, GLA_VDIM, GLA_VDIM, GLA_GATE_RANK,
    LRU_WIDTH, LRU_WIDTH,
    N_BRANCHES * D_MODEL,
)
D_IN = sum(IN_SPLITS)

kernel_name = "hybrid_hgrn2_ssd_gla_rglru_block"

F32 = jnp.float32


def rmsnorm(x, w):
    xf = x.astype(F32)
    y = xf * lax.rsqrt(jnp.mean(xf * xf, axis=-1, keepdims=True) + NORM_EPS)
    return (y * w.astype(F32)).astype(x.dtype)


def causal_dwconv(x, w, b):
    c = x.shape[-1]
    y = lax.conv_general_dilated(
        x, w[:, None, :].astype(x.dtype), window_strides=(1,),
        padding=[(w.shape[0] - 1, 0)], dimension_numbers=("NWC", "WIO", "NWC"),
        feature_group_count=c)
    return y + b.astype(x.dtype)


def chunk_gated_linear_attn(q, k, v, log_f):
    bsz, seqlen, nh, dk = q.shape
    dv = v.shape[-1]
    n = seqlen // CHUNK

    def to_chunks(t):
        return jnp.moveaxis(t.astype(F32).reshape(bsz, n, CHUNK, nh, t.shape[-1]), 1, 0)

    qc, kc, vc, gc = to_chunks(q), to_chunks(k), to_chunks(v), to_chunks(log_f)
    causal = jnp.tril(jnp.ones((CHUNK, CHUNK), bool))

    def step(state, inp):
        qi, ki, vi, gi = inp
        bcum = jnp.cumsum(gi, axis=1)
        rel = bcum[:, :, None] - bcum[:, None, :]
        rel = jnp.where(causal[None, :, :, None, None], rel, -jnp.inf)
        scores = jnp.einsum('bihd,bjhd,bijhd->bhij', qi, ki, jnp.exp(rel))
        o_intra = jnp.einsum('bhij,bjhv->bihv', scores, vi)
        o_inter = jnp.einsum('bihd,bhdv->bihv', qi * jnp.exp(bcum), state)
        b_last = bcum[:, -1:]
        k_dec = ki * jnp.exp(b_last - bcum)
        new_state = state * jnp.exp(b_last[:, 0])[..., None] + jnp.einsum('bjhd,bjhv->bhdv', k_dec, vi)
        return new_state, o_intra + o_inter

    s0 = jnp.zeros((bsz, nh, dk, dv), F32)
    _, o = lax.scan(step, s0, (qc, kc, vc, gc))
    return jnp.moveaxis(o, 0, 1).reshape(bsz, seqlen, nh, dv)


def hgrn2_branch(q, f_pre, i, g, lb, norm_w):
    bsz, seqlen, _ = q.shape
    lbf = lb.astype(F32)
    log_f = jnp.logaddexp(jnp.log(lbf), jnp.log1p(-lbf) + jax.nn.log_sigmoid(f_pre.astype(F32)))
    key = -jnp.expm1(log_f)
    qf = jax.nn.silu(q.astype(F32))
    hk = lambda t: t.reshape(bsz, seqlen, HGRN_HEADS, HGRN_EXPAND)
    hv = lambda t: t.reshape(bsz, seqlen, HGRN_HEADS, HGRN_VHEAD)
    o = chunk_gated_linear_attn(hk(qf), hk(key), hv(i.astype(F32)), hk(log_f))
    o = rmsnorm(o, norm_w) * jax.nn.silu(hv(g.astype(F32)))
    return o.reshape(bsz, seqlen, HGRN_VDIM)


def gla_branch(q, k, v, g, gate_lr, gate_w, gate_b, norm_w):
    bsz, seqlen, _ = q.shape
    log_a = jax.nn.log_sigmoid(gate_lr.astype(F32) @ gate_w.astype(F32) + gate_b.astype(F32)) / GLA_GATE_NORMALIZER
    hk = lambda t: t.astype(F32).reshape(bsz, seqlen, GLA_HEADS, GLA_HEAD_K)
    hv = lambda t: t.astype(F32).reshape(bsz, seqlen, GLA_HEADS, GLA_HEAD_V)
    o = chunk_gated_linear_attn(hk(q) * (GLA_HEAD_K ** -0.5), hk(k), hv(v), hk(log_a))
    o = rmsnorm(o, norm_w) * jax.nn.silu(hv(g))
    return o.reshape(bsz, seqlen, GLA_VDIM)


def ssd_branch(z, xbc, dt_raw, conv_w, conv_b, dt_bias, a_log, d_skip, norm_w):
    bsz, seqlen, _ = z.shape
    n = seqlen // CHUNK
    xbc = jax.nn.silu(causal_dwconv(xbc, conv_w, conv_b).astype(F32))
    xs, bm, cm = jnp.split(xbc, [SSD_DINNER, SSD_DINNER + SSD_GROUPS * SSD_STATE], axis=-1)
    dt = jax.nn.softplus(dt_raw.astype(F32) + dt_bias.astype(F32))
    a_neg = -jnp.exp(a_log.astype(F32)).reshape(SSD_GROUPS, SSD_HPG)
    x_c = xs.reshape(bsz, n, CHUNK, SSD_GROUPS, SSD_HPG, SSD_HEADDIM)
    dt_c = dt.reshape(bsz, n, CHUNK, SSD_GROUPS, SSD_HPG)
    b_c = bm.reshape(bsz, n, CHUNK, SSD_GROUPS, SSD_STATE)
    c_c = cm.reshape(bsz, n, CHUNK, SSD_GROUPS, SSD_STATE)
    a_cs = jnp.cumsum(dt_c * a_neg, axis=2)
    xdt = x_c * dt_c[..., None]
    causal = jnp.tril(jnp.ones((CHUNK, CHUNK), bool))
    seg = a_cs[:, :, :, None] - a_cs[:, :, None, :]
    lmat = jnp.exp(jnp.where(causal[None, None, :, :, None, None], seg, -jnp.inf))
    y_diag = jnp.einsum('bcigs,bcjgs,bcijgh,bcjghp->bcighp', c_c, b_c, lmat, xdt)
    decay_states = jnp.exp(a_cs[:, :, -1:] - a_cs)
    states = jnp.einsum('bcjgs,bcjgh,bcjghp->bcghps', b_c, decay_states, xdt)
    cs = jnp.cumsum(a_cs[:, :, -1], axis=1)
    rel = cs[:, :, None] - cs[:, None, :]
    ctril = jnp.tril(jnp.ones((n, n), bool))
    carried = jnp.einsum('bzcgh,bcghps->bzghps',
                         jnp.exp(jnp.where(ctril[None, :, :, None, None], rel, -jnp.inf)), states)
    prev = jnp.concatenate([jnp.zeros_like(carried[:, :1]), carried[:, :-1]], axis=1)
    y_off = jnp.einsum('bcigs,bcghps,bcigh->bcighp', c_c, prev, jnp.exp(a_cs))
    y = y_diag + y_off + x_c * d_skip.astype(F32).reshape(SSD_GROUPS, SSD_HPG)[..., None]
    y = y.reshape(bsz, seqlen, SSD_DINNER) * jax.nn.silu(z.astype(F32))
    yg = y.reshape(bsz, seqlen, SSD_GROUPS, SSD_DINNER // SSD_GROUPS)
    yg = yg * lax.rsqrt(jnp.mean(yg * yg, axis=-1, keepdims=True) + NORM_EPS)
    return yg.reshape(bsz, seqlen, SSD_DINNER) * norm_w.astype(F32)


def rglru_branch(xb, gate, conv_w, conv_b, wa, ba, wx, bx, lam):
    bsz, seqlen, _ = xb.shape
    u = causal_dwconv(xb, conv_w, conv_b).astype(F32)
    ub = u.reshape(bsz, seqlen, LRU_BLOCKS, LRU_BLOCK)
    r = jax.nn.sigmoid(jnp.einsum('blkd,kde->blke', ub, wa.astype(F32)).reshape(bsz, seqlen, LRU_WIDTH) + ba.astype(F32))
    i = jax.nn.sigmoid(jnp.einsum('blkd,kde->blke', ub, wx.astype(F32)).reshape(bsz, seqlen, LRU_WIDTH) + bx.astype(F32))
    log_a = -LRU_C * r * jax.nn.softplus(-lam.astype(F32))
    a = jnp.exp(log_a)
    bterm = jnp.sqrt(-jnp.expm1(2.0 * log_a)) * (i * u)

    def combine(e1, e2):
        a1, b1 = e1
        a2, b2 = e2
        return a1 * a2, a2 * b1 + b2

    _, h = lax.associative_scan(combine, (a, bterm), axis=1)
    return h * jax.nn.gelu(gate.astype(F32), approximate=True)


def setup_inputs(seed: int = 0) -> dict:
    key = jax.random.key(seed)
    ks = jax.random.split(key, 32)
    nrm = lambda k, shape, scale: jax.random.normal(k, shape, F32) * scale
    gain = lambda k, shape: 1.0 + 0.02 * jax.random.normal(k, shape, F32)
    dt = jnp.exp(jax.random.uniform(ks[8], (DEPTH, SSD_HEADS), F32, math.log(1e-3), math.log(1e-1)))
    lam_u = jax.random.uniform(ks[19], (DEPTH, LRU_WIDTH), F32, 0.9, 0.999) ** (1.0 / LRU_C)
    return {
        "x": nrm(ks[0], (BATCH, SEQ, D_MODEL), 1.0),
        "norm_mix_w": gain(ks[1], (DEPTH, D_MODEL)),
        "w_in": nrm(ks[2], (DEPTH, D_MODEL, D_IN), D_MODEL ** -0.5),
        "hgrn_lower_bounds": nrm(ks[3], (DEPTH, HGRN_FDIM), 0.1),
        "hgrn_norm_w": gain(ks[4], (DEPTH, HGRN_VHEAD)),
        "ssd_conv_w": nrm(ks[5], (DEPTH, CONV_WIDTH, SSD_CONV_DIM), CONV_WIDTH ** -0.5),
        "ssd_conv_b": nrm(ks[6], (DEPTH, SSD_CONV_DIM), 0.01),
        "ssd_dt_bias": dt + jnp.log(-jnp.expm1(-dt)),
        "ssd_a_log": jnp.log(jax.random.uniform(ks[9], (DEPTH, SSD_HEADS), F32, 1.0, 16.0)),
        "ssd_d": gain(ks[10], (DEPTH, SSD_HEADS)),
        "ssd_norm_w": gain(ks[11], (DEPTH, SSD_DINNER)),
        "gla_gate_w": nrm(ks[12], (DEPTH, GLA_GATE_RANK, GLA_KDIM), GLA_GATE_RANK ** -0.5),
        "gla_gate_b": nrm(ks[13], (DEPTH, GLA_KDIM), 0.01),
        "gla_norm_w": gain(ks[14], (DEPTH, GLA_HEAD_V)),
        "lru_conv_w": nrm(ks[15], (DEPTH, CONV_WIDTH, LRU_WIDTH), CONV_WIDTH ** -0.5),
        "lru_conv_b": nrm(ks[16], (DEPTH, LRU_WIDTH), 0.01),
        "lru_wa": nrm(ks[17], (DEPTH, LRU_BLOCKS, LRU_BLOCK, LRU_BLOCK), LRU_BLOCK ** -0.5),
        "lru_ba": nrm(ks[18], (DEPTH, LRU_WIDTH), 0.01),
        "lru_wx": nrm(ks[20], (DEPTH, LRU_BLOCKS, LRU_BLOCK, LRU_BLOCK), LRU_BLOCK ** -0.5),
        "lru_bx": nrm(ks[21], (DEPTH, LRU_WIDTH), 0.01),
        "lru_lambda": jnp.log(lam_u) - jnp.log1p(-lam_u),
        "w_branch": nrm(ks[22], (DEPTH, N_BRANCHES, BRANCH_WIDTH, D_MODEL), BRANCH_WIDTH ** -0.5),
        "w_out": nrm(ks[23], (DEPTH, D_MODEL, D_MODEL), D_MODEL ** -0.5),
        "norm_ffn_w": gain(ks[24], (DEPTH, D_MODEL)),
        "w_ffn_in": nrm(ks[25], (DEPTH, D_MODEL, 2 * D_FF), D_MODEL ** -0.5),
        "w_ffn_out": nrm(ks[26], (DEPTH, D_FF, D_MODEL), D_FF ** -0.5),
        "norm_f_w": gain(ks[27], (D_MODEL,)),
    }


def reference(x, norm_mix_w, w_in, hgrn_lower_bounds, hgrn_norm_w, ssd_conv_w, ssd_conv_b, ssd_dt_bias,
              ssd_a_log, ssd_d, ssd_norm_w, gla_gate_w, gla_gate_b, gla_norm_w, lru_conv_w, lru_conv_b,
              lru_wa, lru_ba, lru_wx, lru_bx, lru_lambda, w_branch, w_out, norm_ffn_w, w_ffn_in,
              w_ffn_out, norm_f_w):
    bsz, seqlen, _ = x.shape
    split_idx = np.cumsum(IN_SPLITS)[:-1].tolist()
    lb_all = jnp.cumsum(jax.nn.softmax(hgrn_lower_bounds.astype(F32), axis=0), axis=0)
    lb_all = lb_all - lb_all[0:1]
    h = x
    for l in range(DEPTH):
        xn = rmsnorm(h, norm_mix_w[l])
        proj = xn @ w_in[l]
        (hq, hf, hi, hg, sz, sxbc, sdt, gq, gk, gv, gg, glr, lx, lg, mg) = jnp.split(proj, split_idx, axis=-1)
        ya = hgrn2_branch(hq, hf, hi, hg, lb_all[l], hgrn_norm_w[l])
        yb = ssd_branch(sz, sxbc, sdt, ssd_conv_w[l], ssd_conv_b[l], ssd_dt_bias[l], ssd_a_log[l],
                        ssd_d[l], ssd_norm_w[l])
        yc = gla_branch(gq, gk, gv, gg, glr, gla_gate_w[l], gla_gate_b[l], gla_norm_w[l])
        yd = rglru_branch(lx, lg, lru_conv_w[l], lru_conv_b[l], lru_wa[l], lru_ba[l], lru_wx[l], lru_bx[l],
                          lru_lambda[l])
        ys = jnp.stack([ya, yb, yc, yd], axis=2).astype(h.dtype)
        branch_out = jnp.einsum('blnw,nwd->blnd', ys, w_branch[l])
        gates = jax.nn.sigmoid(mg.astype(F32)).reshape(bsz, seqlen, N_BRANCHES, D_MODEL)
        merged = jnp.sum(gates * branch_out.astype(F32), axis=2).astype(h.dtype)
        h = h + (merged @ w_out[l]).astype(h.dtype)
        xn = rmsnorm(h, norm_ffn_w[l])
        gate_up = xn @ w_ffn_in[l]
        g_ff, u_ff = jnp.split(gate_up, [D_FF], axis=-1)
        h = h + ((jax.nn.silu(g_ff) * u_ff) @ w_ffn_out[l]).astype(h.dtype)
    return rmsnorm(h, norm_f_w)
```

```python
import numpy as np
from contextlib import ExitStack
import concourse.bass as bass
import concourse.mybir as mybir
from concourse.bass_utils import run_bass_kernel_spmd

F32 = mybir.dt.float32
BF16 = mybir.dt.bfloat16
AF = mybir.ActivationFunctionType
ALU = mybir.AluOpType

T = 512
D = 1024
D_IN = 10008
D_FF = 2816
EPS = 1e-6
DEPTH = 2
C_HQ, C_HF, C_HI, C_HG = 0, 512, 1024, 1536
C_SZ, C_SX, C_SB, C_SC, C_SDT = 2048, 2560, 3072, 3200, 3328
C_GQ, C_GK, C_GV, C_GG, C_GLR = 3336, 3592, 3848, 4360, 4872
C_LX, C_LG, C_MG = 4888, 5400, 5912


class Sched:
    def __init__(self, nc, n_dma_sems=8):
        self.nc = nc
        self.names = ["pe", "act", "dve", "pool", "sp"]
        self.sem = {k: nc.alloc_semaphore(name=f"s_{k}") for k in self.names}
        self.cnt = {k: 0 for k in self.names}
        self.waited = {k: {} for k in self.names}
        self.res = {}
        self.q = {k: [] for k in self.names}
        self.semobj = {("e", k): s for k, s in self.sem.items()}
        self.dsems, self.dcnt, self.dnext = {}, {}, {}
        for q in ("sp", "pool"):
            self.dsems[q] = [nc.alloc_semaphore(name=f"d_{q}_{i}") for i in range(n_dma_sems)]
            self.dcnt[q] = [0] * n_dma_sems
            self.dnext[q] = 0
            for i, s in enumerate(self.dsems[q]):
                self.semobj[("d", q, i)] = s
        self.self_wait = True

    def _wait(self, eng, tok):
        key, val = tok
        if key == ("e", eng) and (eng == "pe" or not self.self_wait):
            return
        if self.waited[eng].get(key, 0) >= val:
            return
        so = self.semobj[key]
        self.q[eng].append(lambda e, so=so, val=val: e.wait_ge(so, val))
        self.waited[eng][key] = val

    def _deps(self, eng, reads, writes):
        toks = []
        for k in reads:
            r = self.res.get(k)
            if r and r["w"]:
                toks.append(r["w"])
        for k in writes:
            r = self.res.get(k)
            if r:
                if r["w"]:
                    toks.append(r["w"])
                toks.extend(r["r"])
        for t in toks:
            self._wait(eng, t)

    def _mark(self, tok, reads, writes):
        for k in reads:
            r = self.res.setdefault(k, {"w": None, "r": []})
            r["r"] = [t for t in r["r"] if t[0] != tok[0]] + [tok]
        for k in writes:
            self.res[k] = {"w": tok, "r": []}

    def op(self, eng, fn, reads=(), writes=()):
        self._deps(eng, reads, writes)
        self.cnt[eng] += 1
        sm = self.sem[eng]
        self.q[eng].append(lambda e, fn=fn, sm=sm: fn(e).then_inc(sm, 1))
        self._mark((("e", eng), self.cnt[eng]), reads, writes)

    def dma(self, eng, out, in_, reads=(), writes=(), war=()):
        i = self.dnext[eng]
        self.dnext[eng] = (i + 1) % len(self.dsems[eng])
        if self.dcnt[eng][i] > 0:
            self._wait(eng, (("d", eng, i), self.dcnt[eng][i]))
        self._deps(eng, reads, list(writes) + list(war))
        self.dcnt[eng][i] += 16
        ds = self.dsems[eng][i]
        self.q[eng].append(lambda e, out=out, in_=in_, ds=ds: e.dma_start(out=out, in_=in_).then_inc(ds, 16))
        self._mark((("d", eng, i), self.dcnt[eng][i]), reads, writes)

    def finish(self, eng, keys):
        for k in keys:
            r = self.res.get(k)
            if r and r["w"]:
                self._wait(eng, r["w"])

    def replay(self):
        with self.nc.Block() as block:
            for name, deco in (("sp", block.sync), ("act", block.scalar), ("pe", block.tensor),
                               ("dve", block.vector), ("pool", block.gpsimd)):
                q = self.q[name]

                def body(e, q=q):
                    for f in q:
                        f(e)
                deco(body)


def _fm(v):
    v = np.asarray(v, np.float32)
    return np.ascontiguousarray(v.reshape(-1, 128).T)


def _rows(v, n):
    o = np.zeros((128, 1), np.float32)
    o[:n, 0] = np.asarray(v, np.float32)
    return o


class _Pack:
    def __init__(self):
        self.parts, self.idx, self.off = [], {}, 0

    def add(self, name, a):
        a = np.asarray(a, np.float32)
        assert a.shape[0] == 128
        self.idx[name] = self.off
        self.parts.append(a)
        self.off += a.shape[1]

    def arr(self):
        return np.ascontiguousarray(np.concatenate(self.parts, axis=1))


def _blockdiag(w):
    o = np.zeros((128, 4, 128), np.float32)
    for k in range(8):
        fc, hb = k // 2, k % 2
        o[hb * 64:(hb + 1) * 64, fc, hb * 64:(hb + 1) * 64] = w[k]
    return o.reshape(128, 512)


def pack_params(inp, depth):
    pk = _Pack()
    for l in range(depth):
        pk.add(f"nmw{l}", _fm(inp["norm_mix_w"][l]))
        pk.add(f"nfw{l}", _fm(inp["norm_ffn_w"][l]))
        pk.add(f"hlb{l}", _fm(inp["hgrn_lower_bounds"][l]))
        pk.add(f"hnw{l}", np.asarray(inp["hgrn_norm_w"][l], np.float32).reshape(128, 1))
        pk.add(f"scw{l}", np.concatenate([_fm(inp["ssd_conv_w"][l][k]) for k in range(4)], axis=1))
        pk.add(f"scb{l}", _fm(inp["ssd_conv_b"][l]))
        pk.add(f"sdtb{l}", _rows(inp["ssd_dt_bias"][l], 8))
        pk.add(f"salog{l}", _rows(inp["ssd_a_log"][l], 8))
        pk.add(f"sd{l}", _fm(np.repeat(np.asarray(inp["ssd_d"][l], np.float32), 64)))
        pk.add(f"snw{l}", _fm(inp["ssd_norm_w"][l]))
        pk.add(f"ggb{l}", _fm(inp["gla_gate_b"][l]))
        pk.add(f"gnw{l}", np.asarray(inp["gla_norm_w"][l], np.float32).reshape(128, 1))
        pk.add(f"lcw{l}", np.concatenate([_fm(inp["lru_conv_w"][l][k]) for k in range(4)], axis=1))
        pk.add(f"lcb{l}", _fm(inp["lru_conv_b"][l]))
        pk.add(f"lba{l}", _fm(inp["lru_ba"][l]))
        pk.add(f"lbx{l}", _fm(inp["lru_bx"][l]))
        pk.add(f"llam{l}", _fm(inp["lru_lambda"][l]))
    pk.add("nf", _fm(inp["norm_f_w"]))
    return pk


def make_consts():
    j = np.arange(128)[:, None]
    i = np.arange(128)[None, :]
    same = (j // 64 == i // 64) & (j <= i)
    ident = np.eye(128, dtype=np.float32)
    mask2 = same.astype(np.float32)
    maskneg = np.where(same, 0.0, -30000.0).astype(np.float32)
    rmask = np.ones((128, T), np.float32)
    rmask[:, ::64] = 0.0
    cst = np.concatenate([ident, np.tile(mask2, (1, 4)), maskneg, rmask], axis=1)
    sel = np.zeros((128, 8, 128), np.float32)
    for h in range(8):
        sel[h, h, :] = 1.0
    selp = np.zeros((128, 4, 128), np.float32)
    for hp in range(4):
        selp[hp, hp, 0:64] = 1.0
        selp[hp + 4, hp, 64:128] = 1.0
    selc = np.concatenate([sel.reshape(128, 1024), selp.reshape(128, 512)], axis=1)[0:8]
    return np.ascontiguousarray(cst), np.ascontiguousarray(selc)


CST_ID, CST_M2, CST_MN, CST_RM = 0, 128, 640, 768
CST_N = 1280


def build(nt, depth, pidx, npar, dbg_names=(), stage=99):
    nc = bass.Bass("TRN2", target_bir_lowering=False)
    L = nt * T

    def din(name, shape):
        return nc.dram_tensor(name, shape, F32, kind="ExternalInput").ap()

    x_d = din("x", [L, D])
    w_in_d = din("w_in", [depth, D, D_IN])
    w_br_d = din("w_branch", [depth, 4, 512, D])
    w_out_d = din("w_out", [depth, D, D])
    w_f1_d = din("w_ffn_in", [depth, D, 2 * D_FF])
    w_f2_d = din("w_ffn_out", [depth, D_FF, D])
    par_d = din("params", [128, npar])
    ggw_d = din("ggw", [16, depth * 256])
    bd_d = din("bd", [128, depth * 2 * 512])
    cst_d = din("cst", [128, CST_N])
    sel_d = din("selc", [8, 1536])
    out_d = nc.dram_tensor("out", [L, D], F32, kind="ExternalOutput").ap()
    dbg_d = {}

    S = Sched(nc)
    with ExitStack() as es:
        def sb(name, shape, dt):
            return es.enter_context(nc.sbuf_tensor(name, shape, dt))

        def psum(name, shape, dt):
            return es.enter_context(nc.psum_tensor(name, shape, dt))

        h = sb("h", [128, 8, T], F32)
        xn = sb("xn", [128, 8, T], BF16)
        ys = sb("ys", [128, 16, T], BF16)
        FF = sb("FF", [128, 20, T], F32)
        BB = sb("BB", [128, 20, T], BF16)
        G = sb("G", [128, 4, T], F32)
        WB = [sb(f"W{k}", [128, 4096], BF16) for k in range(4)]
        par = sb("par", [128, npar], F32)
        ggw = sb("ggwt", [16, depth * 256], F32)
        bd = sb("bdt", [128, depth * 2 * 512], BF16)
        cst = sb("cstt", [128, CST_N], F32)
        selc = sb("selt", [8, 512], F32)
        XIN = sb("XIN", [128, 1024], F32)
        idb = sb("idb", [128, 128], BF16)
        ones = sb("ones", [128, 128], BF16)
        hm = sb("hm", [128, 2], F32)
        mnb = sb("mnb", [128, 128], BF16)
        selb = sb("selb", [8, 1024], BF16)
        sqt = sb("sqt", [128, 2, T], BF16)
        rs4 = sb("rs4", [128, 4, T], F32)
        tmpf = sb("tmpf", [128, 2, T], F32)
        STb = sb("STb", [128, 4, 4, 128], BF16)
        XB = sb("XB", [128, 2, T + 3], F32)
        LT = sb("LT", [128, 2, T], F32)
        MTb = sb("MTb", [128, 2, 8, 128], BF16)
        dtk = sb("dtk", [128, 4, 16], F32)
        decs = sb("decs", [128, 4, 8], F32)
        HSf = sb("HSf", [128, depth, 4, 128], F32)
        HSb = sb("HSb", [128, depth, 4, 128], BF16)
        GSf = sb("GSf", [128, depth, 2, 128], F32)
        GSb = sb("GSb", [128, depth, 4, 128], BF16)
        SSf = sb("SSf", [128, depth, 4, 64], F32)
        SSb = sb("SSb", [128, depth, 4, 64], BF16)
        stail = sb("stail", [128, depth, 6, 3], F32)
        ltail = sb("ltail", [128, depth, 4, 3], F32)
        lcar = sb("lcar", [128, depth, 4], F32)
        lbt = sb("lbt", [128, depth, 4], F32)
        omlb = sb("omlb", [128, depth, 4], F32)
        aneg = sb("aneg", [8, depth], F32)
        nggb = sb("nggb", [128, depth, 2], F32)
        lc1 = sb("lc1", [128, depth, 4], F32)
        lc2 = sb("lc2", [128, depth, 4], F32)
        P = [psum(f"P{k}", [128, T], F32) for k in range(8)]
        Pbf = [p[:].bitcast(BF16) for p in P]

        FFflat = FF[:].rearrange("p a t -> p (a t)")
        BBflat = BB[:].rearrange("p a t -> p (a t)")
        HID = FFflat[:, 8 * T:20 * T].bitcast(BF16).rearrange("p (j t) -> p j t", t=T)
        IO = [BBflat[:, (12 + 4 * k) * T:(16 + 4 * k) * T].bitcast(F32) for k in range(2)]

        ident = cst[:, CST_ID:CST_ID + 128]
        mask2x4 = cst[:, CST_M2:CST_M2 + 512].rearrange("p (a b) -> p a b", b=128)
        maskneg = cst[:, CST_MN:CST_MN + 128]
        rmask = cst[:, CST_RM:CST_RM + T]
        selp = selc[:, 0:512].rearrange("p (h m) -> p h m", m=128)

        def PK(b):
            return [("P", b)]

        def kFF(*idx):
            return [("FF", i) for i in idx]

        def kBB(*idx):
            return [("BB", i) for i in idx]

        def kIO(k):
            return [("BB", 12 + 4 * k + i) for i in range(4)]

        def kHID(j):
            return [("FF", 8 + j // 2)]

        def MM(out, lhsT, rhs, start, stop, r, w):
            S.op("pe", lambda e: e.matmul(out, lhsT, rhs, start=start, stop=stop), reads=r, writes=w)

        def TR(out, in_, idn, r, w):
            S.op("pe", lambda e: e.transpose(out, in_, idn), reads=r, writes=w)

        def ACT(out, in_, func, r, w, bias=None, scale=None):
            kw = {}
            if bias is not None:
                kw["bias"] = bias
            if scale is not None:
                kw["scale"] = scale
            S.op("act", lambda e: e.activation(out=out, in_=in_, func=func, **kw), reads=r, writes=w)

        def TT(out, a, b, op, r, w):
            S.op("dve", lambda e: e.tensor_tensor(out=out, in0=a, in1=b, op=op), reads=r, writes=w)

        def TS(out, a, s1, s2, op0, op1, r, w):
            if s2 is None:
                S.op("dve", lambda e: e.tensor_scalar(out=out, in0=a, scalar1=s1, scalar2=None, op0=op0), reads=r, writes=w)
            else:
                S.op("dve", lambda e: e.tensor_scalar(out=out, in0=a, scalar1=s1, scalar2=s2, op0=op0, op1=op1), reads=r, writes=w)

        def STT(out, in0, scalar, in1, op0, op1, r, w):
            S.op("dve", lambda e: e.scalar_tensor_tensor(out=out, in0=in0, scalar=scalar, in1=in1, op0=op0, op1=op1),
                 reads=r, writes=w)

        def SCAN(out, d0, d1, init, r, w):
            S.op("dve", lambda e: e.tensor_tensor_scan(out=out, data0=d0, data1=d1, initial=init, op0=ALU.mult, op1=ALU.add),
                 reads=r, writes=w)

        def CP(out, in_, r, w):
            S.op("dve", lambda e: e.tensor_copy(out=out, in_=in_), reads=r, writes=w)

        def MEMSET(ap, val, w):
            S.op("dve", lambda e: e.memset(ap, val), writes=w)

        def PC(name):
            return pidx[name]

        def pcol(name, c, n=1, rows=slice(0, 128)):
            o = pidx[name] + c
            return par[rows, o:o + n]

        rot = {"A": [0, 1, 2, 3], "lo": [0, 1], "mid": [2, 3]}
        rpos = {k: 0 for k in rot}

        def bank(pool):
            b = rot[pool][rpos[pool] % len(rot[pool])]
            rpos[pool] += 1
            return b

        wpos = [0]
        wring = [4]

        def wbuf():
            k = (wpos[0] + 1) % wring[0]
            wpos[0] = k
            return k

        NWT = 48 * depth
        wscr = nc.dram_tensor("wscr", [NWT, 128, 4096], BF16).ap()
        wst = {"seq": 0, "t": 0}

        def load_cols(src2d, nk, col0, ncols, k=None, dcol0=0, width=None, final=True):
            if k is None:
                k = wbuf()
            if width is None:
                width = ncols
            n = nk * width
            wv = WB[k][:, 0:n].rearrange("p (kc c) -> p kc c", c=width)
            step = max(1, nk // 2)
            wid = wst["seq"]
            if wst["t"] == 0:
                src = src2d[:, col0:col0 + ncols].rearrange("(kc p) c -> p kc c", p=128)
                for k0 in range(0, nk, step):
                    k1 = min(nk, k0 + step)
                    S.dma("pool", wv[:, k0:k1, dcol0:dcol0 + ncols], src[:, k0:k1, :],
                          writes=[("W", k, kc) for kc in range(k0, k1)], war=[("Wall", k)])
                if final:
                    S.dma("sp", wscr[wid][:, 0:n], WB[k][:, 0:n], reads=[("W", k, kc) for kc in range(nk)] + [("Wall", k)],
                          writes=[("wscr", wid)])
            elif final:
                for k0 in range(0, nk, step):
                    k1 = min(nk, k0 + step)
                    S.dma("pool", WB[k][:, k0 * width:k1 * width], wscr[wid][:, k0 * width:k1 * width], reads=[("wscr", wid)],
                          writes=[("W", k, kc) for kc in range(k0, k1)], war=[("Wall", k)])
            if final:
                wst["seq"] += 1
                assert wst["seq"] <= NWT
            return k, wv

        def proj_fm(k, wv, c0, m, pool="A"):
            b = bank(pool)
            for kc in range(8):
                MM(P[b][0:m, :], wv[:, kc, c0:c0 + m], xn[:, kc, :], kc == 0, kc == 7,
                   [("W", k, kc), ("Wall", k), ("xn", kc)], PK(b))
            return b

        def proj_tm(k, wv, c0, n, blk, pool="A"):
            b = bank(pool)
            for kc in range(8):
                MM(P[b][:, 0:n], xn[:, kc, blk * 128:(blk + 1) * 128], wv[:, kc, c0:c0 + n], kc == 0, kc == 7,
                   [("W", k, kc), ("Wall", k), ("xn", kc)], PK(b))
            return b

        S.dma("sp", par[:], par_d, writes=["par"])
        S.dma("sp", ggw[:], ggw_d, writes=["ggw"])
        S.dma("sp", cst[:], cst_d, writes=["cst"])
        S.dma("sp", selc[:], sel_d[:, 1024:1536], writes=["sel"])
        S.dma("pool", bd[:], bd_d, writes=["bd"])
        S.dma("pool", idb[:], cst_d[:, CST_ID:CST_ID + 128], writes=["idb"])
        S.dma("pool", mnb[:], cst_d[:, CST_MN:CST_MN + 128], writes=["mnb"])
        S.dma("pool", selb[:], sel_d[:, 0:1024], writes=["selb"])
        MEMSET(ones[:], 1.0, ["ones"])
        MEMSET(hm[:], 0.0, ["hm"])
        MEMSET(hm[0:64, 0:1], 1.0, ["hm"])
        MEMSET(hm[64:128, 1:2], 1.0, ["hm"])
        for nm, tns in (("HSf", HSf), ("HSb", HSb), ("GSf", GSf), ("GSb", GSb), ("SSf", SSf), ("SSb", SSb),
                        ("stail", stail), ("ltail", ltail), ("lcar", lcar), ("lbt", lbt)):
            MEMSET(tns[:], 0.0, [nm])
        if depth == 2:
            TT(lbt[:, 1, :], pcol("hlb1", 0, 4), pcol("hlb0", 0, 4), ALU.subtract, ["par", "lbt"], ["lbt"])
            ACT(lbt[:, 1, :], lbt[:, 1, :], AF.Sigmoid, ["lbt"], ["lbt"])
        TS(omlb[:], lbt[:], -1.0, 1.0, ALU.mult, ALU.add, ["lbt"], ["omlb"])
        for l in range(depth):
            ACT(aneg[:, l:l + 1], pcol(f"salog{l}", 0, 1, slice(0, 8)), AF.Exp, ["par"], ["aneg"])
            TS(aneg[:, l:l + 1], aneg[:, l:l + 1], -1.0, None, ALU.mult, None, ["aneg"], ["aneg"])
            TS(nggb[:, l, :], pcol(f"ggb{l}", 0, 2), -1.0, None, ALU.mult, None, ["par"], ["nggb"])
            ACT(lc1[:, l, :], pcol(f"llam{l}", 0, 4), AF.Exp, ["par"], ["lc1"], scale=-1.0)
            ACT(lc1[:, l, :], lc1[:, l, :], AF.Ln, ["lc1"], ["lc1"], bias=1.0)
            TS(lc2[:, l, :], lc1[:, l, :], -16.0, None, ALU.mult, None, ["lc1"], ["lc2"])
            TS(lc1[:, l, :], lc1[:, l, :], -8.0, None, ALU.mult, None, ["lc1", "lc2"], ["lc1"])
        CK = ["par", "cst", "sel", "ggw", "bd", "idb", "ones", "lbt", "omlb", "aneg", "nggb", "lc1", "lc2"]

        def dump(name, ap, shape, keys):
            if name in dbg_names and name not in dbg_d:
                d = nc.dram_tensor("dbg_" + name, list(shape), F32, kind="ExternalOutput").ap()
                dbg_d[name] = d
                S.dma("pool", d, ap, reads=keys, writes=[("dbg", name)])

        sq_state = {"pending": None, "cnt": 0}

        def sq_feed(m):
            s = sq_state["cnt"] % 2
            prev = sq_state["pending"]
            if prev is not None:
                pm, ps_ = prev
                MM(P[7][:, :], ones[:], sqt[:, ps_, :], pm == 0, False, [("sqt", ps_), "ones"], PK(7))
            ACT(sqt[:, s, :], h[:, m, :], AF.Square, [("h", m)], [("sqt", s)])
            sq_state["pending"] = (sq_state["cnt"], s)
            sq_state["cnt"] += 1

        def sq_flush():
            pm, ps_ = sq_state["pending"]
            MM(P[7][:, :], ones[:], sqt[:, ps_, :], pm == 0, True, [("sqt", ps_), "ones"], PK(7))
            sq_state["pending"], sq_state["cnt"] = None, 0

        def rmsnorm(wname, dst_fn, dkeys_fn, pre=False):
            if pre:
                sq_flush()
                b = 7
            else:
                b = bank("A")
                for m in range(8):
                    s = m % 2
                    ACT(sqt[:, s, :], h[:, m, :], AF.Square, [("h", m)], [("sqt", s)])
                    MM(P[b][:, :], ones[:], sqt[:, s, :], m == 0, m == 7, [("sqt", s), "ones"], PK(b))
            ACT(rs4[:, 0, :], P[b][:, :], AF.Ln, PK(b), [("rs4", 0)], bias=EPS, scale=1.0 / D)
            ACT(rs4[:, 0, :], rs4[:, 0, :], AF.Exp, [("rs4", 0)], [("rs4", 0)], scale=-0.5)
            for m in range(8):
                STT(dst_fn(m), h[:, m, :], pcol(wname, m), rs4[:, 0, :], ALU.mult, ALU.mult,
                    [("h", m), ("rs4", 0), "par"], dkeys_fn(m))

        UFv = FFflat[:, 0:8 * T].rearrange("p (h c v) -> p h c v", h=4, c=8)
        SNv = FFflat[:, 12 * T:16 * T].bitcast(BF16).rearrange("p (h c v) -> p h c v", h=4, c=8)

        def gla_core(l, dk, Qi, Ki, Ei, Sf, Sb, skey, nw_name, ysbase, stage=99, mid_work=None, late_work=None):
            def hp(hd):
                fc = hd if dk == 128 else hd // 2
                pr = slice(0, 128) if dk == 128 else slice((hd % 2) * 64, (hd % 2) * 64 + 64)
                return fc, pr
            if dk == 64:
                MEMSET(FF[:, 12:16, :], 0.0, kFF(12, 13, 14, 15))
            for blk in range(4):
                for cc in range(2):
                    c = 2 * blk + cc
                    rows = slice(cc * 64, cc * 64 + 64)
                    bU = bank("mid")
                    for hd in range(4):
                        fc, pr = hp(hd)
                        if dk == 128:
                            kt = BB[rows, 16 + blk, hd * 128:(hd + 1) * 128]
                            ktk = kBB(16 + blk)
                        else:
                            kt = BB[rows, 16 + blk // 2, (blk % 2) * 256 + hd * 64:(blk % 2) * 256 + hd * 64 + 64]
                            ktk = kBB(16 + blk // 2)
                        MM(P[bU][pr, hd * 128:(hd + 1) * 128], kt, BB[rows, 12 + blk, hd * 128:(hd + 1) * 128],
                           True, True, ktk + kBB(12 + blk), PK(bU))
                    if dk == 128:
                        ACT(UFv[:, :, c, :], P[bU][:, :].rearrange("p (h v) -> p h v", v=128), AF.Copy, PK(bU), kFF(*range(8)))
                    else:
                        for q in range(2):
                            prq = slice(q * 64, q * 64 + 64)
                            ACT(UFv[prq, q::2, c, :], P[bU][prq, :].rearrange("p (h v) -> p h v", v=128)[:, q::2, :], AF.Copy,
                                PK(bU), kFF(*range(8)))
            for blk in range(4):
                bc = slice(blk * 128, (blk + 1) * 128)
                bS = bank("lo")
                for hd in range(4):
                    fc, pr = hp(hd)
                    if dk == 128:
                        MM(P[bS][:, hd * 128:(hd + 1) * 128], BB[:, Ki + fc, bc], BB[:, Qi + fc, bc], True, True,
                           kBB(Ki + fc, Qi + fc), PK(bS))
                    else:
                        MM(P[bS][:, hd * 128:(hd + 1) * 128], BB[:, Ki + fc, bc], BB[:, 6 + hd, bc], True, True,
                           kBB(Ki + fc, 6 + hd), PK(bS))
                TT(STb[:, blk, :, :], P[bS][:, :].rearrange("p (a b) -> p a b", b=128), mask2x4, ALU.mult,
                   PK(bS) + ["cst"], [("ST", blk)])
            for c in range(8):
                for hd in range(4):
                    fc, pr = hp(hd)
                    sfv = Sf[:, l, hd, :] if dk == 128 else Sf[pr, l, fc, :]
                    prev = sfv if c == 0 else UFv[pr, hd, c - 1, :]
                    STT(UFv[pr, hd, c, :], prev, FF[pr, Ei + fc, c * 64 + 63:c * 64 + 64], UFv[pr, hd, c, :], ALU.mult, ALU.add,
                        [(skey, l, hd, "f")] + kFF(2 * hd, 2 * hd + 1, Ei + fc), kFF(2 * hd, 2 * hd + 1))
            for hd in range(4):
                fc, pr = hp(hd)
                sfv = Sf[:, l, hd, :] if dk == 128 else Sf[pr, l, fc, :]
                ACT(SNv[pr, hd, 1:8, :], UFv[pr, hd, 0:7, :], AF.Copy, kFF(2 * hd, 2 * hd + 1), kFF(12 + hd))
                CP(sfv, UFv[pr, hd, 7, :], kFF(2 * hd, 2 * hd + 1), [(skey, l, hd, "f")])
            if mid_work is not None:
                mid_work()
            if late_work is not None:
                late_work()
            for blk in range(4):
                for cc in range(2):
                    c = 2 * blk + cc
                    rows = slice(cc * 64, cc * 64 + 64)
                    ccols = slice(c * 64, c * 64 + 64)
                    for hd in range(4):
                        fc, pr = hp(hd)
                        MM(P[4 + hd][:, ccols], BB[rows, 12 + blk, hd * 128:(hd + 1) * 128],
                           STb[rows, blk, hd, cc * 64:cc * 64 + 64], True, False,
                           kBB(12 + blk) + [("ST", blk)], PK(4 + hd))
                        qi = Qi + fc if dk == 128 else 6 + hd
                        if c == 0:
                            MM(P[4 + hd][:, ccols], Sb[:, l, hd, :], BB[:, qi, ccols], False, True,
                               [(skey, l, hd, "b")] + kBB(qi), PK(4 + hd))
                        else:
                            MM(P[4 + hd][:, ccols], SNv[:, hd, c, :], BB[:, qi, ccols], False, True,
                               kFF(12 + hd) + kBB(qi), PK(4 + hd))
            for hd in range(4):
                fc, pr = hp(hd)
                ACT(Sb[pr, l, hd, :], UFv[pr, hd, 7, :], AF.Copy, kFF(2 * hd, 2 * hd + 1), [(skey, l, hd, "b")])
            nb = [bank("lo"), bank("lo"), bank("mid"), bank("mid")]
            for hd in range(4):
                s = hd % 2
                ACT(sqt[:, s, :], P[4 + hd][:, :], AF.Square, PK(4 + hd), [("sqt", s)])
                MM(P[nb[hd]][:, :], ones[:], sqt[:, s, :], True, True, [("sqt", s), "ones"], PK(nb[hd]))
            for hd in range(4):
                ACT(rs4[:, hd, :], P[nb[hd]][:, :], AF.Ln, PK(nb[hd]), [("rs4", hd)], bias=EPS, scale=1.0 / 128)
            for hd in range(4):
                ACT(rs4[:, hd, :], rs4[:, hd, :], AF.Exp, [("rs4", hd)], [("rs4", hd)], scale=-0.5)
            for hd in range(4):
                s = hd % 2
                TT(tmpf[:, s, :], P[4 + hd][:, :], rs4[:, hd, :], ALU.mult, PK(4 + hd) + [("rs4", hd)], [("tmpf", s)])
                STT(ys[:, ysbase + hd, :], tmpf[:, s, :], pcol(nw_name, 0), G[:, hd, :], ALU.mult, ALU.mult,
                    [("tmpf", s), ("G", hd), "par"], [("ys", ysbase + hd)])

        def vtok_and_ktok(l, col_v, Kdi, dk):
            k, wv = load_cols(w_in_d[l], 8, col_v, 512)
            for blk in range(4):
                b = proj_tm(k, wv, 0, 512, blk)
                ACT(BB[:, 12 + blk, :], P[b][:, :], AF.Copy, PK(b), kBB(12 + blk))
            nfc = 4 if dk == 128 else 2
            for blk in range(4):
                b = bank("A")
                for fc in range(nfc):
                    TR(Pbf[b][:, fc * 128:(fc + 1) * 128], BB[:, Kdi + fc, blk * 128:(blk + 1) * 128], idb[:],
                       kBB(Kdi + fc) + ["idb"], PK(b))
                if dk == 128:
                    CP(BB[:, 16 + blk, :], Pbf[b][:, 0:512], PK(b), kBB(16 + blk))
                else:
                    CP(BB[:, 16 + blk // 2, (blk % 2) * 256:(blk % 2) * 256 + 256], Pbf[b][:, 0:256], PK(b),
                       kBB(16 + blk // 2))

        def conv4(pb, m, tail, l, ci, wname, bname, nci, dst, dkeys, func, tmp=None, tkeys=None, evac_dve=False):
            s = ci % 2
            if tmp is None:
                tmp, tkeys = tmpf[0:m, s, :], [("tmpf", s)]
            if evac_dve:
                CP(XB[0:m, s, 3:3 + T], P[pb][0:m, :], PK(pb), [("XB", s)])
            else:
                ACT(XB[0:m, s, 3:3 + T], P[pb][0:m, :], AF.Copy, PK(pb), [("XB", s)])
            CP(XB[0:m, s, 0:3], tail[0:m, l, ci, :], [("tail", wname, l, ci), ("XB", s)], [("XB", s)])
            TS(tmp, XB[0:m, s, 3:3 + T], pcol(wname, 3 * nci + ci), pcol(bname, ci), ALU.mult, ALU.add,
               [("XB", s), "par"], tkeys)
            for kk in range(3):
                STT(tmp, XB[0:m, s, kk:kk + T], pcol(wname, kk * nci + ci), tmp, ALU.mult, ALU.add,
                    [("XB", s), "par"] + tkeys, tkeys)
            CP(tail[0:m, l, ci, :], XB[0:m, s, T:T + 3], [("XB", s)], [("tail", wname, l, ci)])
            ACT(dst, tmp, func, tkeys, dkeys)

        def layer(l, t):
            rmsnorm(f"nmw{l}", lambda m: xn[:, m, :], lambda m: [("xn", m)], pre=(l > 0))
            dump(f"xn{l}", xn[:], [128, 8, T], [("xn", m) for m in range(8)])
            if stage < 1:
                return
            k, wv = load_cols(w_in_d[l], 8, C_HQ, 512)
            for hd in range(4):
                b = proj_fm(k, wv, hd * 128, 128)
                ACT(FF[:, hd, :], P[b][:, :], AF.Silu, PK(b), kFF(hd))
            if stage < 1.1:
                return
            k, wv = load_cols(w_in_d[l], 8, C_HF, 512)
            for hd in range(4):
                b = proj_fm(k, wv, hd * 128, 128)
                ACT(FF[:, 4 + hd, :], P[b][:, :], AF.Sigmoid, PK(b), kFF(4 + hd))
            for hd in range(4):
                TS(FF[:, 4 + hd, :], FF[:, 4 + hd, :], omlb[:, l, hd:hd + 1], lbt[:, l, hd:hd + 1], ALU.mult, ALU.add,
                   kFF(4 + hd) + ["omlb", "lbt"], kFF(4 + hd))
            for hd in range(4):
                ACT(FF[:, 8 + hd, :], FF[:, 4 + hd, :], AF.Ln, kFF(4 + hd), kFF(8 + hd))
            for hd in range(4):
                TS(FF[:, 4 + hd, :], FF[:, 4 + hd, :], -1.0, 1.0, ALU.mult, ALU.add, kFF(4 + hd), kFF(4 + hd))
                SCAN(FF[:, 12 + hd, :], rmask, FF[:, 8 + hd, :], 0.0, kFF(8 + hd) + ["cst"], kFF(12 + hd))
            for hd in range(4):
                ACT(FF[:, 8 + hd, :], FF[:, 12 + hd, :], AF.Exp, kFF(12 + hd), kFF(8 + hd))
                ACT(FF[:, 16 + hd, :], FF[:, 12 + hd, :], AF.Exp, kFF(12 + hd), kFF(16 + hd), scale=-1.0)
            for hd in range(4):
                TT(BB[:, hd, :], FF[:, hd, :], FF[:, 8 + hd, :], ALU.mult, kFF(hd, 8 + hd), kBB(hd))
                TT(BB[:, 4 + hd, :], FF[:, 4 + hd, :], FF[:, 16 + hd, :], ALU.mult, kFF(4 + hd, 16 + hd), kBB(4 + hd))
                ev = FF[:, 8 + hd, :].rearrange("p (c j) -> p c j", j=64)[:, :, 63:64].to_broadcast([128, 8, 64])
                TT(BB[:, 8 + hd, :].rearrange("p (c j) -> p c j", j=64),
                   BB[:, 4 + hd, :].rearrange("p (c j) -> p c j", j=64), ev, ALU.mult,
                   kBB(4 + hd) + kFF(8 + hd), kBB(8 + hd))
            dump(f"einv{l}", FF[:, 16:20, :], [128, 4, T], kFF(16, 17, 18, 19))
            dump(f"bcum{l}", FF[:, 12:16, :], [128, 4, T], kFF(12, 13, 14, 15))
            dump(f"kt{l}", BB[:, 4:8, :], [128, 4, T], kBB(4, 5, 6, 7))
            dump(f"qt{l}", BB[:, 0:4, :], [128, 4, T], kBB(0, 1, 2, 3))
            if stage < 1.2:
                return
            vtok_and_ktok(l, C_HI, 8, 128)
            if stage < 1.3:
                return
            def hg_work():
                k, wv = load_cols(w_in_d[l], 8, C_HG, 512)
                for hd in range(4):
                    b = proj_fm(k, wv, hd * 128, 128, pool="A")
                    ACT(G[:, hd, :], P[b][:, :], AF.Silu, PK(b), [("G", hd)])
            def gla_gate():
                k, wv = load_cols(w_in_d[l], 8, C_GLR, 16)
                b = proj_fm(k, wv, 0, 16, pool="lo")
                ACT(tmpf[0:16, 0, :], P[b][0:16, :], AF.Copy, PK(b), [("tmpf", 0)])
                gb = [bank("lo"), bank("lo")]
                for fc in range(2):
                    MM(P[gb[fc]][:, :], ggw[0:16, l * 256 + fc * 128:l * 256 + (fc + 1) * 128], tmpf[0:16, 0, :], True, True,
                       [("tmpf", 0), "ggw"], PK(gb[fc]))
                for fc in range(2):
                    ACT(FF[:, 16 + fc, :], P[gb[fc]][:, :], AF.Exp, PK(gb[fc]) + ["nggb"], kFF(16 + fc), bias=nggb[:, l, fc:fc + 1], scale=-1.0)
                for fc in range(2):
                    ACT(FF[:, 16 + fc, :], FF[:, 16 + fc, :], AF.Ln, kFF(16 + fc), kFF(16 + fc), bias=1.0)
                for fc in range(2):
                    SCAN(FF[:, 18 + fc, :], rmask, FF[:, 16 + fc, :], 0.0, kFF(16 + fc) + ["cst"], kFF(18 + fc))
                for fc in range(2):
                    ACT(FF[:, 8 + fc, :], FF[:, 18 + fc, :], AF.Exp, kFF(18 + fc), kFF(8 + fc), scale=-1.0 / 16)
                    ACT(FF[:, 10 + fc, :], FF[:, 18 + fc, :], AF.Exp, kFF(18 + fc), kFF(10 + fc), scale=1.0 / 16)
            gla_core(l, 128, 0, 4, 8, HSf, HSb, "HS", f"hnw{l}", 0, stage=stage, mid_work=hg_work,
                     late_work=gla_gate if stage >= 2 else None)
            dump(f"ya{l}", ys[:, 0:4, :], [128, 4, T], [("ys", i) for i in range(4)])

            if stage < 2:
                return
            stage_save = stage
            if stage < 2.2:
                return
            k, wv = load_cols(w_in_d[l], 8, C_GQ, 512)
            for fc in range(2):
                b = proj_fm(k, wv, fc * 128, 128)
                STT(BB[:, fc, :], P[b][:, :], 0.125, FF[:, 8 + fc, :], ALU.mult, ALU.mult, PK(b) + kFF(8 + fc), kBB(fc))
            for fc in range(2):
                b = proj_fm(k, wv, 256 + fc * 128, 128)
                TT(BB[:, 2 + fc, :], P[b][:, :], FF[:, 10 + fc, :], ALU.mult, PK(b) + kFF(10 + fc), kBB(2 + fc))
                ev = FF[:, 8 + fc, :].rearrange("p (c j) -> p c j", j=64)[:, :, 63:64].to_broadcast([128, 8, 64])
                TT(BB[:, 4 + fc, :].rearrange("p (c j) -> p c j", j=64),
                   BB[:, 2 + fc, :].rearrange("p (c j) -> p c j", j=64), ev, ALU.mult,
                   kBB(2 + fc) + kFF(8 + fc), kBB(4 + fc))
            if stage < 2.3:
                return
            if stage < 2.4:
                return
            for hd in range(4):
                TS(BB[:, 6 + hd, :], BB[:, hd // 2, :], hm[:, hd % 2:hd % 2 + 1], None, ALU.mult, None,
                   kBB(hd // 2) + ["hm"], kBB(6 + hd))
            vtok_and_ktok(l, C_GV, 4, 64)
            if stage < 2.5:
                return
            def gg_work():
                k, wv = load_cols(w_in_d[l], 8, C_GG, 512)
                for hd in range(4):
                    b = proj_fm(k, wv, hd * 128, 128, pool="A")
                    ACT(G[:, hd, :], P[b][:, :], AF.Silu, PK(b), [("G", hd)])
            def ssd_prelude():
                k, wv = load_cols(w_in_d[l], 8, C_SX, 512)
                for fc in range(4):
                    b = proj_fm(k, wv, fc * 128, 128, pool="lo")
                    conv4(b, 128, stail, l, fc, f"scw{l}", f"scb{l}", 6, FF[:, 16 + fc, :], kFF(16 + fc), AF.Silu,
                          tmp=FF[:, 10 + fc % 2, :], tkeys=kFF(10 + fc % 2), evac_dve=True)
                k, wv = load_cols(w_in_d[l], 8, C_SB, 256)
                for ci in (4, 5):
                    b = proj_fm(k, wv, (ci - 4) * 128, 128, pool="lo")
                    conv4(b, 128, stail, l, ci, f"scw{l}", f"scb{l}", 6, BB[:, 6 + ci, :], kBB(6 + ci), AF.Silu,
                          tmp=FF[:, 10 + ci % 2, :], tkeys=kFF(10 + ci % 2), evac_dve=True)
            gla_core(l, 64, 0, 2, 8, GSf, GSb, "GS", f"gnw{l}", 8, stage=stage - 1.2, mid_work=gg_work,
                     late_work=ssd_prelude if stage >= 3 else None)
            dump(f"yc{l}", ys[:, 8:12, :], [128, 4, T], [("ys", i) for i in range(8, 12)])

            if stage < 3:
                return
            XBF = [4, 5, 12, 13]
            for fc in range(4):
                ACT(BB[:, XBF[fc], :], FF[:, 16 + fc, :], AF.Copy, kFF(16 + fc), kBB(XBF[fc]))
            k, wv = load_cols(w_in_d[l], 8, C_SDT, 8)
            b = proj_fm(k, wv, 0, 8)
            R8 = slice(0, 8)
            ACT(FF[R8, 5, :], P[b][R8, :], AF.Exp, PK(b) + ["par"], kFF(5), bias=pcol(f"sdtb{l}", 0, 1, R8))
            ACT(FF[R8, 5, :], FF[R8, 5, :], AF.Ln, kFF(5), kFF(5), bias=1.0)
            TS(FF[R8, 6, :], FF[R8, 5, :], aneg[:, l:l + 1], None, ALU.mult, None, kFF(5) + ["aneg"], kFF(6))
            SCAN(FF[R8, 7, :], rmask[R8, :], FF[R8, 6, :], 0.0, kFF(6) + ["cst"], kFF(7))
            TS(FF[R8, 6, :], FF[R8, 7, :], -1.0, None, ALU.mult, None, kFF(7), kFF(6))
            AH = FFflat[0:8, 10 * T:11 * T].bitcast(BF16).rearrange("p (a t) -> p a t", t=T)
            NH = FFflat[0:8, 11 * T:12 * T].bitcast(BF16).rearrange("p (a t) -> p a t", t=T)
            ACT(AH[:, 0, :], FF[R8, 7, :], AF.Copy, kFF(7), kFF(10))
            TT(AH[:, 1, :], FF[R8, 7, :], AH[:, 0, :], ALU.subtract, kFF(7, 10), kFF(10))
            TS(NH[:, 0, :], AH[:, 0, :], -1.0, None, ALU.mult, None, kFF(10), kFF(11))
            TS(NH[:, 1, :], AH[:, 1, :], -1.0, None, ALU.mult, None, kFF(10), kFF(11))
            acs3 = FF[R8, 7, :].rearrange("p (c j) -> p c j", j=64)
            TT(FF[R8, 8, :].rearrange("p (c j) -> p c j", j=64), acs3[:, :, 63:64].to_broadcast([8, 8, 64]), acs3, ALU.subtract,
               kFF(7), kFF(8))
            ACT(FF[R8, 8, :], FF[R8, 8, :], AF.Exp, kFF(8), kFF(8))
            ACT(FF[R8, 9, 0:8], FF[R8, 7, :].rearrange("p (c j) -> p c j", j=64)[:, :, 63], AF.Exp, kFF(7), kFF(9))
            kz, wvz = load_cols(w_in_d[l], 8, C_SZ, 512)
            for fc in range(4):
                bz = proj_fm(kz, wvz, fc * 128, 128)
                ACT(G[:, fc, :], P[bz][:, :], AF.Silu, PK(bz), [("G", fc)])
            for hp in range(4):
                b = bank("A")
                MM(P[b][:, 0:8], selp[:, hp, :], FF[R8, 9, 0:8], True, True, kFF(9) + ["sel"], PK(b))
                CP(decs[:, hp, :], P[b][:, 0:8], PK(b), [("decs", hp)])
                b = bank("A")
                MM(P[b][:, :], selp[:, hp, :], FF[R8, 7, :], True, True, kFF(7) + ["sel"], PK(b))
                s = hp % 2
                ACT(tmpf[:, s, :], P[b][:, :], AF.Exp, PK(b), [("tmpf", s)])
                TT(BB[:, 15 + hp, :], BB[:, 11, :], tmpf[:, s, :], ALU.mult, kBB(11) + [("tmpf", s)], kBB(15 + hp))
            for blk in range(4):
                bc = slice(blk * 128, (blk + 1) * 128)
                b = bank("A")
                TR(P[b][:, 0:8], FF[R8, 5, bc], ident[0:8, 0:8], kFF(5) + ["cst"], PK(b))
                TR(P[b][:, 8:16], FF[R8, 8, bc], ident[0:8, 0:8], kFF(8) + ["cst"], PK(b))
                CP(dtk[:, blk, :], P[b][:, 0:16], PK(b), [("dtk", blk)])
                b = bank("A")
                for fc in range(4):
                    TR(Pbf[b][:, fc * 128:(fc + 1) * 128], BB[:, XBF[fc], bc], idb[:], kBB(XBF[fc]) + ["idb"], PK(b))
                TT(BB[:, 6 + blk, :].rearrange("p (h q) -> p h q", q=64), Pbf[b][:, 0:512].rearrange("p (h q) -> p h q", q=64),
                   dtk[:, blk, 0:8].unsqueeze(2).to_broadcast([128, 8, 64]), ALU.mult, PK(b) + [("dtk", blk)], kBB(6 + blk))
                TT(BB[:, blk, :].rearrange("p (h q) -> p h q", q=64), BB[:, 6 + blk, :].rearrange("p (h q) -> p h q", q=64),
                   dtk[:, blk, 8:16].unsqueeze(2).to_broadcast([128, 8, 64]), ALU.mult, kBB(6 + blk) + [("dtk", blk)], kBB(blk))
                b = bank("A")
                TR(Pbf[b][:, 0:128], BB[:, 10, bc], idb[:], kBB(10) + ["idb"], PK(b))
                CP(BB[:, 14, bc], Pbf[b][:, 0:128], PK(b), kBB(14))
            rpos["A"] = 2
            for g in range(2):
                gr = slice(g * 64, g * 64 + 64)
                for blk in range(4):
                    bc = slice(blk * 128, (blk + 1) * 128)
                    MM(P[g][:, bc], BB[gr, 10, bc], BB[gr, 11, bc], True, True, kBB(10, 11), PK(g))
            UFs = FFflat[:, 12 * T:16 * T].rearrange("p (c h q) -> p c h q", c=8, h=4)
            SNs = FFflat[:, 0:2 * T].bitcast(BF16).rearrange("p (c h q) -> p c h q", c=8, h=4)
            for blk in range(4):
                for cc in range(2):
                    c = 2 * blk + cc
                    rows = slice(cc * 64, cc * 64 + 64)
                    bU = bank("mid")
                    for hh in range(8):
                        g, hp = hh // 4, hh % 4
                        gr = slice(g * 64, g * 64 + 64)
                        MM(P[bU][gr, hp * 64:(hp + 1) * 64], BB[rows, 14, blk * 128 + g * 64:blk * 128 + g * 64 + 64],
                           BB[rows, blk, hh * 64:(hh + 1) * 64], True, True, kBB(14, blk), PK(bU))
                    ACT(UFs[:, c, :, :], P[bU][:, 0:256].rearrange("p (h q) -> p h q", q=64), AF.Copy, PK(bU), kFF(12, 13, 14, 15))
            for c in range(8):
                for hp in range(4):
                    prev = SSf[:, l, hp, :] if c == 0 else UFs[:, c - 1, hp, :]
                    STT(UFs[:, c, hp, :], prev, decs[:, hp, c:c + 1], UFs[:, c, hp, :], ALU.mult, ALU.add,
                        [("SSf", l), ("decs", hp)] + kFF(12, 13, 14, 15), kFF(12, 13, 14, 15))
            ACT(SNs[:, 1:8, :, :], UFs[:, 0:7, :, :], AF.Copy, kFF(12, 13, 14, 15), kFF(0, 1))
            CP(SSf[:, l, :, :], UFs[:, 7, :, :], kFF(12, 13, 14, 15), [("SSf", l)])

            def ssd_mt(blk):
                bc = slice(blk * 128, (blk + 1) * 128)
                s = blk % 2
                for g in range(2):
                    bL = bank("mid")
                    for q in range(4):
                        hh = 4 * g + q
                        qs_ = slice(q * 128, (q + 1) * 128)
                        sb_h = selb[:, hh * 128:(hh + 1) * 128]
                        MM(P[bL][:, qs_], sb_h, AH[:, 0, bc], True, False, kFF(10) + ["selb"], PK(bL))
                        MM(P[bL][:, qs_], sb_h, AH[:, 1, bc], False, False, kFF(10) + ["selb"], PK(bL))
                        MM(P[bL][:, qs_], NH[:, 0, bc], sb_h, False, False, kFF(11) + ["selb"], PK(bL))
                        MM(P[bL][:, qs_], NH[:, 1, bc], sb_h, False, False, kFF(11) + ["selb"], PK(bL))
                        MM(P[bL][:, qs_], idb[:], mnb[:], False, True, ["idb", "mnb"], PK(bL))
                    ACT(LT[:, g, :], P[bL][:, :], AF.Exp, PK(bL), [("LT", g)])
                    TT(MTb[:, s, 4 * g:4 * g + 4, :], LT[:, g, :].rearrange("p (q i) -> p q i", i=128),
                       P[g][:, bc].unsqueeze(1).to_broadcast([128, 4, 128]), ALU.mult,
                       [("P", g), ("LT", g)], [("MT", s, hh_) for hh_ in range(4 * g, 4 * g + 4)])

            def ssd_out(blk):
                s = blk % 2
                for cc in range(2):
                    c = 2 * blk + cc
                    ccols = slice(c * 64, c * 64 + 64)
                    for hh in range(8):
                        g, hp, fc = hh // 4, hh % 4, hh // 2
                        gr = slice(g * 64, g * 64 + 64)
                        pr = slice((hh % 2) * 64, (hh % 2) * 64 + 64)
                        MM(P[4 + fc][pr, ccols], BB[:, 6 + blk, hh * 64:(hh + 1) * 64], MTb[:, s, hh, cc * 64:cc * 64 + 64],
                           True, False, kBB(6 + blk) + [("MT", s, hh)], PK(4 + fc))
                        if c == 0:
                            MM(P[4 + fc][pr, ccols], SSb[gr, l, hp, :], BB[gr, 15 + hp, ccols], False, True,
                               [("SSb", l)] + kBB(15 + hp), PK(4 + fc))
                        else:
                            MM(P[4 + fc][pr, ccols], SNs[gr, c, hp, :], BB[gr, 15 + hp, ccols], False, True,
                               kFF(0, 1) + kBB(15 + hp), PK(4 + fc))

            ssd_mt(0)
            for blk in range(4):
                if blk + 1 < 4:
                    ssd_mt(blk + 1)
                ssd_out(blk)
            ACT(SSb[:, l, :, :], UFs[:, 7, :, :], AF.Copy, kFF(12, 13, 14, 15), [("SSb", l)])
            for fc in range(4):
                s = fc % 2
                STT(tmpf[:, s, :], FF[:, 16 + fc, :], pcol(f"sd{l}", fc), P[4 + fc][:, :], ALU.mult, ALU.add,
                    kFF(16 + fc) + PK(4 + fc) + ["par"], [("tmpf", s)])
                TT(FF[:, 8 + fc, :], tmpf[:, s, :], G[:, fc, :], ALU.mult, [("tmpf", s), ("G", fc)], kFF(8 + fc))
            for g in range(2):
                b = bank("mid")
                for q in range(2):
                    fc = 2 * g + q
                    ACT(sqt[:, q, :], FF[:, 8 + fc, :], AF.Square, kFF(8 + fc), [("sqt", q)])
                    MM(P[b][:, :], ones[:], sqt[:, q, :], q == 0, q == 1, [("sqt", q), "ones"], PK(b))
                ACT(rs4[:, g, :], P[b][:, :], AF.Ln, PK(b), [("rs4", g)], bias=EPS, scale=1.0 / 256)
            for g in range(2):
                ACT(rs4[:, g, :], rs4[:, g, :], AF.Exp, [("rs4", g)], [("rs4", g)], scale=-0.5)
            for g in range(2):
                for q in range(2):
                    fc = 2 * g + q
                    STT(ys[:, 4 + fc, :], FF[:, 8 + fc, :], pcol(f"snw{l}", fc), rs4[:, g, :], ALU.mult, ALU.mult,
                        kFF(8 + fc) + [("rs4", g), "par"], [("ys", 4 + fc)])
            dump(f"yb{l}", ys[:, 4:8, :], [128, 4, T], [("ys", i) for i in range(4, 8)])

            if stage < 4:
                return
            lru_w = {}

            def lru_unit_a(fc):
                if fc == 0:
                    lru_w["x"] = load_cols(w_in_d[l], 8, C_LX, 512, k=3)
                k, wv = lru_w["x"]
                b = proj_fm(k, wv, fc * 128, 128)
                gs = fc % 2
                conv4(b, 128, ltail, l, fc, f"lcw{l}", f"lcb{l}", 4, FF[:, 8 + fc, :], kFF(8 + fc), AF.Copy,
                      tmp=G[:, gs, :], tkeys=[("G", gs)])
                ACT(BB[:, fc, :], FF[:, 8 + fc, :], AF.Copy, kFF(8 + fc), kBB(fc))

            def lru_unit_a2(fc):
                g2 = 2 + fc % 2
                b = bank("A")
                MM(P[b][:, :], bd[:, (l * 2 + 0) * 512 + fc * 128:(l * 2 + 0) * 512 + (fc + 1) * 128], BB[:, fc, :], True, True,
                   kBB(fc) + ["bd"], PK(b))
                ACT(FF[:, 12 + fc, :], P[b][:, :], AF.Sigmoid, PK(b) + ["par"], kFF(12 + fc), bias=pcol(f"lba{l}", fc))
                b = bank("A")
                MM(P[b][:, :], bd[:, (l * 2 + 1) * 512 + fc * 128:(l * 2 + 1) * 512 + (fc + 1) * 128], BB[:, fc, :], True, True,
                   kBB(fc) + ["bd"], PK(b))
                ACT(FF[:, 16 + fc, :], P[b][:, :], AF.Sigmoid, PK(b) + ["par"], kFF(16 + fc), bias=pcol(f"lbx{l}", fc))
                ACT(G[:, g2, :], FF[:, 12 + fc, :], AF.Exp, kFF(12 + fc) + ["lc2"], [("G", g2)], scale=lc2[:, l, fc:fc + 1])
                ACT(FF[:, 12 + fc, :], FF[:, 12 + fc, :], AF.Exp, kFF(12 + fc) + ["lc1"], kFF(12 + fc), scale=lc1[:, l, fc:fc + 1])
                ACT(G[:, g2, :], G[:, g2, :], AF.Sqrt, [("G", g2)], [("G", g2)], bias=1.0, scale=-1.0)
                TT(FF[:, 16 + fc, :], FF[:, 16 + fc, :], FF[:, 8 + fc, :], ALU.mult, kFF(16 + fc, 8 + fc), kFF(16 + fc))
                TT(FF[:, 16 + fc, :], FF[:, 16 + fc, :], G[:, g2, :], ALU.mult, kFF(16 + fc) + [("G", g2)], kFF(16 + fc))
                SCAN(FF[:, 8 + fc, :], FF[:, 12 + fc, :], FF[:, 16 + fc, :], lcar[:, l, fc:fc + 1],
                     kFF(12 + fc, 16 + fc) + [("lcar", l, fc)], kFF(8 + fc))
                CP(lcar[:, l, fc:fc + 1], FF[:, 8 + fc, T - 1:T], kFF(8 + fc), [("lcar", l, fc)])

            def lru_unit_b(fc):
                if fc == 0:
                    lru_w["g"] = load_cols(w_in_d[l], 8, C_LG, 512, k=3)
                k, wv = lru_w["g"]
                b = proj_fm(k, wv, fc * 128, 128)
                tg, tk = FF[:, 12 + fc, :], kFF(12 + fc)
                ACT(tg, P[b][:, :], AF.Square, PK(b), tk)
                TS(tg, tg, 0.044715, 1.0, ALU.mult, ALU.add, tk, tk)
                TT(tg, tg, P[b][:, :], ALU.mult, tk + PK(b), tk)
                ACT(tg, tg, AF.Sigmoid, tk, tk, scale=1.5957691216057308)
                TT(tg, tg, P[b][:, :], ALU.mult, tk + PK(b), tk)
                TT(ys[:, 12 + fc, :], tg, FF[:, 8 + fc, :], ALU.mult, tk + kFF(8 + fc), [("ys", 12 + fc)])

            lru_units = []
            for fc in range(4):
                lru_units += [(lru_unit_a, fc), (lru_unit_a2, fc)]
            lru_units += [(lru_unit_b, fc) for fc in range(4)]
            if stage < 4.5:
                lru_units = []

            if lru_units:
                wring[0] = 3
            for n in range(4):
                kb, wvb = load_cols(w_br_d[l, n], 4, 0, 1024)
                for half in range(2):
                    kg, wvg = load_cols(w_in_d[l], 8, C_MG + n * 1024 + half * 512, 512)
                    for mm_ in range(4):
                        m = half * 4 + mm_
                        pbk = bank("A")
                        for kc in range(4):
                            MM(P[pbk][:, :], wvb[:, kc, m * 128:(m + 1) * 128], ys[:, n * 4 + kc, :], kc == 0, kc == 3,
                               [("W", kb, kc), ("Wall", kb), ("ys", n * 4 + kc)], PK(pbk))
                        pg = proj_fm(kg, wvg, mm_ * 128, 128)
                        s = m % 2
                        ACT(tmpf[:, s, :], P[pg][:, :], AF.Sigmoid, PK(pg), [("tmpf", s)])
                        if n == 0:
                            TT(FF[:, m, :], tmpf[:, s, :], P[pbk][:, :], ALU.mult, [("tmpf", s)] + PK(pbk), kFF(m))
                        else:
                            TT(tmpf[:, s, :], tmpf[:, s, :], P[pbk][:, :], ALU.mult, [("tmpf", s)] + PK(pbk), [("tmpf", s)])
                            if n < 3:
                                TT(FF[:, m, :], FF[:, m, :], tmpf[:, s, :], ALU.add, kFF(m) + [("tmpf", s)], kFF(m))
                            else:
                                TT(BB[:, 4 + m, :], FF[:, m, :], tmpf[:, s, :], ALU.add, kFF(m) + [("tmpf", s)], kBB(4 + m))
                        if n < 3 and (n * 8 + m) % 2 == 1 and lru_units:
                            fn_, fc_ = lru_units.pop(0)
                            fn_(fc_)
                            if not lru_units:
                                wring[0] = 4
            for half in range(2):
                ko, wvo = load_cols(w_out_d[l], 8, half * 512, 512)
                for mm_ in range(4):
                    m = half * 4 + mm_
                    b = bank("A")
                    for kc in range(8):
                        MM(P[b][:, :], wvo[:, kc, mm_ * 128:(mm_ + 1) * 128], BB[:, 4 + kc, :], kc == 0, kc == 7,
                           [("W", ko, kc), ("Wall", ko)] + kBB(4 + kc), PK(b))
                    TT(h[:, m, :], h[:, m, :], P[b][:, :], ALU.add, [("h", m)] + PK(b), [("h", m)])
                    sq_feed(m)
            dump(f"hmix{l}", h[:], [128, 8, T], [("h", m) for m in range(8)])

            if stage < 6:
                return
            rmsnorm(f"nfw{l}", lambda m: xn[:, m, :], lambda m: [("xn", m)], pre=True)
            for jg in range(11):
                kf = wbuf()
                load_cols(w_f1_d[l], 8, jg * 256, 256, k=kf, dcol0=0, width=512, final=False)
                _, wvf = load_cols(w_f1_d[l], 8, D_FF + jg * 256, 256, k=kf, dcol0=256, width=512)
                for jj in range(2):
                    j = 2 * jg + jj
                    pg = proj_fm(kf, wvf, jj * 128, 128)
                    pu = proj_fm(kf, wvf, 256 + jj * 128, 128)
                    s = j % 2
                    ACT(tmpf[:, s, :], P[pg][:, :], AF.Silu, PK(pg), [("tmpf", s)])
                    TT(HID[:, j, :], tmpf[:, s, :], P[pu][:, :], ALU.mult, [("tmpf", s)] + PK(pu), kHID(j))
            for cg in range(4):
                k2 = []
                for kh in range(2):
                    kk_, wv2 = load_cols(w_f2_d[l][kh * 1408:(kh + 1) * 1408, :], 11, cg * 256, 256)
                    k2.append((kk_, wv2))
                for mm_ in range(2):
                    m = cg * 2 + mm_
                    b = bank("A")
                    for j in range(22):
                        kk_, wv2 = k2[j // 11]
                        MM(P[b][:, :], wv2[:, j % 11, mm_ * 128:(mm_ + 1) * 128], HID[:, j, :], j == 0, j == 21,
                           [("W", kk_, j % 11), ("Wall", kk_)] + kHID(j), PK(b))
                    TT(h[:, m, :], h[:, m, :], P[b][:, :], ALU.add, [("h", m)] + PK(b), [("h", m)])
                    sq_feed(m)
            dump(f"hout{l}", h[:], [128, 8, T], [("h", m) for m in range(8)])

        for t in range(nt):
            t0 = t * T
            wst["seq"], wst["t"] = 0, t
            if t == 0:
                S.dma("sp", XIN[:], x_d[0:128, :], writes=["XIN"])
            for blk in range(4):
                s = blk % 2
                if blk == 0:
                    xsrc, xk = XIN, ["XIN"]
                else:
                    xsrc, xk = IO[s], kIO(s)
                    S.dma("sp", IO[s], x_d[t0 + blk * 128:t0 + (blk + 1) * 128, :], writes=kIO(s))
                for hf in range(2):
                    b = bank("A")
                    for q in range(4):
                        m = hf * 4 + q
                        TR(P[b][:, q * 128:(q + 1) * 128], xsrc[:, m * 128:(m + 1) * 128], ident, xk + ["cst"], PK(b))
                    ACT(h[:, hf * 4:(hf + 1) * 4, blk * 128:(blk + 1) * 128], P[b][:, :].rearrange("p (a b) -> p a b", b=128),
                        AF.Copy, PK(b), [("h", hf * 4 + q) for q in range(4)])
            if t + 1 < nt:
                S.dma("sp", XIN[:], x_d[t0 + T:t0 + T + 128, :], writes=["XIN"])
            for l in range(depth):
                if stage >= 0:
                    layer(l, t)
            rmsnorm("nf", lambda m: FF[:, m, :], lambda m: kFF(m), pre=True)
            for blk in range(4):
                s = blk % 2
                for hf in range(2):
                    b = bank("A")
                    for q in range(4):
                        m = hf * 4 + q
                        TR(P[b][:, q * 128:(q + 1) * 128], FF[:, m, blk * 128:(blk + 1) * 128], ident, kFF(m) + ["cst"], PK(b))
                    ACT(IO[s][:, hf * 512:(hf + 1) * 512], P[b][:, :], AF.Copy, PK(b), kIO(s))
                S.dma("sp", out_d[t0 + blk * 128:t0 + (blk + 1) * 128, :], IO[s], reads=kIO(s), writes=[("out", t, blk)])
        S.finish("sp", [("out", t, blk) for t in range(nt) for blk in range(4)] + [("dbg", n) for n in dbg_d])
        S.replay()
    return nc, S


def prep_shared(inputs, depth):
    pk = pack_params(inputs, depth)
    cst, selc = make_consts()
    ggw = np.concatenate([np.asarray(inputs["gla_gate_w"][l], np.float32) for l in range(depth)], axis=1)
    bd = np.concatenate([np.concatenate([_blockdiag(np.asarray(inputs["lru_wa"][l], np.float32)),
                                         _blockdiag(np.asarray(inputs["lru_wx"][l], np.float32))], axis=1)
                         for l in range(depth)], axis=1)
    shared = {
        "w_in": np.ascontiguousarray(inputs["w_in"][:depth], dtype=np.float32),
        "w_branch": np.ascontiguousarray(inputs["w_branch"][:depth], dtype=np.float32),
        "w_out": np.ascontiguousarray(inputs["w_out"][:depth], dtype=np.float32),
        "w_ffn_in": np.ascontiguousarray(inputs["w_ffn_in"][:depth], dtype=np.float32),
        "w_ffn_out": np.ascontiguousarray(inputs["w_ffn_out"][:depth], dtype=np.float32),
        "params": pk.arr(), "ggw": np.ascontiguousarray(ggw), "bd": np.ascontiguousarray(bd),
        "cst": cst, "selc": selc,
    }
    return pk, shared


def kernel(**inputs):
    x = np.asarray(inputs["x"], np.float32)
    B, L, _ = x.shape
    pk, shared = prep_shared(inputs, DEPTH)
    nc, _ = build(L // T, DEPTH, pk.idx, pk.off)
    in_maps = [dict(shared, x=np.ascontiguousarray(x[b])) for b in range(B)]
    res = run_bass_kernel_spmd(nc, in_maps, core_ids=list(range(B)))
    return np.stack([r["out"] for r in res.results], axis=0).astype(np.float32)
```

```python
import numpy as np
from contextlib import ExitStack
import concourse.bass as bass
import concourse.mybir as mybir
from concourse.bass_utils import run_bass_kernel_spmd

F32 = mybir.dt.float32
BF16 = mybir.dt.bfloat16
AF = mybir.ActivationFunctionType
ALU = mybir.AluOpType

T = 512
D = 1024
D_IN = 10008
D_FF = 2816
EPS = 1e-6
DEPTH = 2
C_HQ, C_HF, C_HI, C_HG = 0, 512, 1024, 1536
C_SZ, C_SX, C_SB, C_SC, C_SDT = 2048, 2560, 3072, 3200, 3328
C_GQ, C_GK, C_GV, C_GG, C_GLR = 3336, 3592, 3848, 4360, 4872
C_LX, C_LG, C_MG = 4888, 5400, 5912


class Sched:
    def __init__(self, nc, n_dma_sems=8):
        self.nc = nc
        self.names = ["pe", "act", "dve", "pool", "sp"]
        self.sem = {k: nc.alloc_semaphore(name=f"s_{k}") for k in self.names}
        self.cnt = {k: 0 for k in self.names}
        self.waited = {k: {} for k in self.names}
        self.res = {}
        self.q = {k: [] for k in self.names}
        self.semobj = {("e", k): s for k, s in self.sem.items()}
        self.dsems, self.dcnt, self.dnext = {}, {}, {}
        for q in ("sp", "pool"):
            self.dsems[q] = [nc.alloc_semaphore(name=f"d_{q}_{i}") for i in range(n_dma_sems)]
            self.dcnt[q] = [0] * n_dma_sems
            self.dnext[q] = 0
            for i, s in enumerate(self.dsems[q]):
                self.semobj[("d", q, i)] = s
        self.self_wait = True

    def _wait(self, eng, tok):
        key, val = tok
        if key == ("e", eng) and (eng == "pe" or not self.self_wait):
            return
        if self.waited[eng].get(key, 0) >= val:
            return
        so = self.semobj[key]
        self.q[eng].append(lambda e, so=so, val=val: e.wait_ge(so, val))
        self.waited[eng][key] = val

    def _deps(self, eng, reads, writes):
        toks = []
        for k in reads:
            r = self.res.get(k)
            if r and r["w"]:
                toks.append(r["w"])
        for k in writes:
            r = self.res.get(k)
            if r:
                if r["w"]:
                    toks.append(r["w"])
                toks.extend(r["r"])
        for t in toks:
            self._wait(eng, t)

    def _mark(self, tok, reads, writes):
        for k in reads:
            r = self.res.setdefault(k, {"w": None, "r": []})
            r["r"] = [t for t in r["r"] if t[0] != tok[0]] + [tok]
        for k in writes:
            self.res[k] = {"w": tok, "r": []}

    def op(self, eng, fn, reads=(), writes=()):
        self._deps(eng, reads, writes)
        self.cnt[eng] += 1
        sm = self.sem[eng]
        self.q[eng].append(lambda e, fn=fn, sm=sm: fn(e).then_inc(sm, 1))
        self._mark((("e", eng), self.cnt[eng]), reads, writes)

    def dma(self, eng, out, in_, reads=(), writes=(), war=()):
        i = self.dnext[eng]
        self.dnext[eng] = (i + 1) % len(self.dsems[eng])
        if self.dcnt[eng][i] > 0:
            self._wait(eng, (("d", eng, i), self.dcnt[eng][i]))
        self._deps(eng, reads, list(writes) + list(war))
        self.dcnt[eng][i] += 16
        ds = self.dsems[eng][i]
        self.q[eng].append(lambda e, out=out, in_=in_, ds=ds: e.dma_start(out=out, in_=in_).then_inc(ds, 16))
        self._mark((("d", eng, i), self.dcnt[eng][i]), reads, writes)

    def finish(self, eng, keys):
        for k in keys:
            r = self.res.get(k)
            if r and r["w"]:
                self._wait(eng, r["w"])

    def replay(self):
        with self.nc.Block() as block:
            for name, deco in (("sp", block.sync), ("act", block.scalar), ("pe", block.tensor),
                               ("dve", block.vector), ("pool", block.gpsimd)):
                q = self.q[name]

                def body(e, q=q):
                    for f in q:
                        f(e)
                deco(body)


def _fm(v):
    v = np.asarray(v, np.float32)
    return np.ascontiguousarray(v.reshape(-1, 128).T)


def _rows(v, n):
    o = np.zeros((128, 1), np.float32)
    o[:n, 0] = np.asarray(v, np.float32)
    return o


class _Pack:
    def __init__(self):
        self.parts, self.idx, self.off = [], {}, 0

    def add(self, name, a):
        a = np.asarray(a, np.float32)
        assert a.shape[0] == 128
        self.idx[name] = self.off
        self.parts.append(a)
        self.off += a.shape[1]

    def arr(self):
        return np.ascontiguousarray(np.concatenate(self.parts, axis=1))


def _blockdiag(w):
    o = np.zeros((128, 4, 128), np.float32)
    for k in range(8):
        fc, hb = k // 2, k % 2
        o[hb * 64:(hb + 1) * 64, fc, hb * 64:(hb + 1) * 64] = w[k]
    return o.reshape(128, 512)


def pack_params(inp, depth):
    pk = _Pack()
    for l in range(depth):
        pk.add(f"nmw{l}", _fm(inp["norm_mix_w"][l]))
        pk.add(f"nfw{l}", _fm(inp["norm_ffn_w"][l]))
        pk.add(f"hlb{l}", _fm(inp["hgrn_lower_bounds"][l]))
        pk.add(f"hnw{l}", np.asarray(inp["hgrn_norm_w"][l], np.float32).reshape(128, 1))
        pk.add(f"scw{l}", np.concatenate([_fm(inp["ssd_conv_w"][l][k]) for k in range(4)], axis=1))
        pk.add(f"scb{l}", _fm(inp["ssd_conv_b"][l]))
        pk.add(f"sdtb{l}", _rows(inp["ssd_dt_bias"][l], 8))
        pk.add(f"salog{l}", _rows(inp["ssd_a_log"][l], 8))
        pk.add(f"sd{l}", _fm(np.repeat(np.asarray(inp["ssd_d"][l], np.float32), 64)))
        pk.add(f"snw{l}", _fm(inp["ssd_norm_w"][l]))
        pk.add(f"ggb{l}", _fm(inp["gla_gate_b"][l]))
        pk.add(f"gnw{l}", np.asarray(inp["gla_norm_w"][l], np.float32).reshape(128, 1))
        pk.add(f"lcw{l}", np.concatenate([_fm(inp["lru_conv_w"][l][k]) for k in range(4)], axis=1))
        pk.add(f"lcb{l}", _fm(inp["lru_conv_b"][l]))
        pk.add(f"lba{l}", _fm(inp["lru_ba"][l]))
        pk.add(f"lbx{l}", _fm(inp["lru_bx"][l]))
        pk.add(f"llam{l}", _fm(inp["lru_lambda"][l]))
    pk.add("nf", _fm(inp["norm_f_w"]))
    return pk


def make_consts():
    j = np.arange(128)[:, None]
    i = np.arange(128)[None, :]
    same = (j // 64 == i // 64) & (j <= i)
    ident = np.eye(128, dtype=np.float32)
    mask2 = same.astype(np.float32)
    maskneg = np.where(same, 0.0, -30000.0).astype(np.float32)
    rmask = np.ones((128, T), np.float32)
    rmask[:, ::64] = 0.0
    cst = np.concatenate([ident, np.tile(mask2, (1, 4)), maskneg, rmask], axis=1)
    sel = np.zeros((128, 8, 128), np.float32)
    for h in range(8):
        sel[h, h, :] = 1.0
    selp = np.zeros((128, 4, 128), np.float32)
    for hp in range(4):
        selp[hp, hp, 0:64] = 1.0
        selp[hp + 4, hp, 64:128] = 1.0
    selc = np.concatenate([sel.reshape(128, 1024), selp.reshape(128, 512)], axis=1)[0:8]
    return np.ascontiguousarray(cst), np.ascontiguousarray(selc)


CST_ID, CST_M2, CST_MN, CST_RM = 0, 128, 640, 768
CST_N = 1280


def build(nt, depth, pidx, npar, dbg_names=(), stage=99):
    nc = bass.Bass("TRN2", target_bir_lowering=False)
    L = nt * T

    def din(name, shape):
        return nc.dram_tensor(name, shape, F32, kind="ExternalInput").ap()

    x_d = din("x", [L, D])
    w_in_d = din("w_in", [depth, D, D_IN])
    w_br_d = din("w_branch", [depth, 4, 512, D])
    w_out_d = din("w_out", [depth, D, D])
    w_f1_d = din("w_ffn_in", [depth, D, 2 * D_FF])
    w_f2_d = din("w_ffn_out", [depth, D_FF, D])
    par_d = din("params", [128, npar])
    ggw_d = din("ggw", [16, depth * 256])
    bd_d = din("bd", [128, depth * 2 * 512])
    cst_d = din("cst", [128, CST_N])
    sel_d = din("selc", [8, 1536])
    out_d = nc.dram_tensor("out", [L, D], F32, kind="ExternalOutput").ap()
    dbg_d = {}

    S = Sched(nc)
    with ExitStack() as es:
        def sb(name, shape, dt):
            return es.enter_context(nc.sbuf_tensor(name, shape, dt))

        def psum(name, shape, dt):
            return es.enter_context(nc.psum_tensor(name, shape, dt))

        h = sb("h", [128, 8, T], F32)
        xn = sb("xn", [128, 8, T], BF16)
        ys = sb("ys", [128, 16, T], BF16)
        FF = sb("FF", [128, 20, T], F32)
        BB = sb("BB", [128, 20, T], BF16)
        G = sb("G", [128, 4, T], F32)
        WB = [sb(f"W{k}", [128, 4096], BF16) for k in range(4)]
        par = sb("par", [128, npar], F32)
        ggw = sb("ggwt", [16, depth * 256], F32)
        bd = sb("bdt", [128, depth * 2 * 512], BF16)
        cst = sb("cstt", [128, CST_N], F32)
        selc = sb("selt", [8, 512], F32)
        XIN = sb("XIN", [128, 1024], F32)
        idb = sb("idb", [128, 128], BF16)
        ones = sb("ones", [128, 128], BF16)
        hm = sb("hm", [128, 2], F32)
        mnb = sb("mnb", [128, 128], BF16)
        selb = sb("selb", [8, 1024], BF16)
        sqt = sb("sqt", [128, 2, T], BF16)
        rs4 = sb("rs4", [128, 4, T], F32)
        tmpf = sb("tmpf", [128, 2, T], F32)
        STb = sb("STb", [128, 4, 4, 128], BF16)
        XB = sb("XB", [128, 2, T + 3], F32)
        LT = sb("LT", [128, 2, T], F32)
        MTb = sb("MTb", [128, 2, 8, 128], BF16)
        dtk = sb("dtk", [128, 4, 16], F32)
        decs = sb("decs", [128, 4, 8], F32)
        HSf = sb("HSf", [128, depth, 4, 128], F32)
        HSb = sb("HSb", [128, depth, 4, 128], BF16)
        GSf = sb("GSf", [128, depth, 2, 128], F32)
        GSb = sb("GSb", [128, depth, 4, 128], BF16)
        SSf = sb("SSf", [128, depth, 4, 64], F32)
        SSb = sb("SSb", [128, depth, 4, 64], BF16)
        stail = sb("stail", [128, depth, 6, 3], F32)
        ltail = sb("ltail", [128, depth, 4, 3], F32)
        lcar = sb("lcar", [128, depth, 4], F32)
        lbt = sb("lbt", [128, depth, 4], F32)
        omlb = sb("omlb", [128, depth, 4], F32)
        aneg = sb("aneg", [8, depth], F32)
        nggb = sb("nggb", [128, depth, 2], F32)
        lc1 = sb("lc1", [128, depth, 4], F32)
        lc2 = sb("lc2", [128, depth, 4], F32)
        P = [psum(f"P{k}", [128, T], F32) for k in range(8)]
        Pbf = [p[:].bitcast(BF16) for p in P]

        FFflat = FF[:].rearrange("p a t -> p (a t)")
        BBflat = BB[:].rearrange("p a t -> p (a t)")
        HID = FFflat[:, 8 * T:20 * T].bitcast(BF16).rearrange("p (j t) -> p j t", t=T)
        IO = [BBflat[:, (12 + 4 * k) * T:(16 + 4 * k) * T].bitcast(F32) for k in range(2)]

        ident = cst[:, CST_ID:CST_ID + 128]
        mask2x4 = cst[:, CST_M2:CST_M2 + 512].rearrange("p (a b) -> p a b", b=128)
        maskneg = cst[:, CST_MN:CST_MN + 128]
        rmask = cst[:, CST_RM:CST_RM + T]
        selp = selc[:, 0:512].rearrange("p (h m) -> p h m", m=128)

        def PK(b):
            return [("P", b)]

        def kFF(*idx):
            return [("FF", i) for i in idx]

        def kBB(*idx):
            return [("BB", i) for i in idx]

        def kIO(k):
            return [("BB", 12 + 4 * k + i) for i in range(4)]

        def kHID(j):
            return [("FF", 8 + j // 2)]

        def MM(out, lhsT, rhs, start, stop, r, w):
            S.op("pe", lambda e: e.matmul(out, lhsT, rhs, start=start, stop=stop), reads=r, writes=w)

        def TR(out, in_, idn, r, w):
            S.op("pe", lambda e: e.transpose(out, in_, idn), reads=r, writes=w)

        def ACT(out, in_, func, r, w, bias=None, scale=None):
            kw = {}
            if bias is not None:
                kw["bias"] = bias
            if scale is not None:
                kw["scale"] = scale
            S.op("act", lambda e: e.activation(out=out, in_=in_, func=func, **kw), reads=r, writes=w)

        def TT(out, a, b, op, r, w):
            S.op("dve", lambda e: e.tensor_tensor(out=out, in0=a, in1=b, op=op), reads=r, writes=w)

        def TS(out, a, s1, s2, op0, op1, r, w):
            if s2 is None:
                S.op("dve", lambda e: e.tensor_scalar(out=out, in0=a, scalar1=s1, scalar2=None, op0=op0), reads=r, writes=w)
            else:
                S.op("dve", lambda e: e.tensor_scalar(out=out, in0=a, scalar1=s1, scalar2=s2, op0=op0, op1=op1), reads=r, writes=w)

        def STT(out, in0, scalar, in1, op0, op1, r, w):
            S.op("dve", lambda e: e.scalar_tensor_tensor(out=out, in0=in0, scalar=scalar, in1=in1, op0=op0, op1=op1),
                 reads=r, writes=w)

        def SCAN(out, d0, d1, init, r, w):
            S.op("dve", lambda e: e.tensor_tensor_scan(out=out, data0=d0, data1=d1, initial=init, op0=ALU.mult, op1=ALU.add),
                 reads=r, writes=w)

        def CP(out, in_, r, w):
            S.op("dve", lambda e: e.tensor_copy(out=out, in_=in_), reads=r, writes=w)

        def MEMSET(ap, val, w):
            S.op("dve", lambda e: e.memset(ap, val), writes=w)

        def PC(name):
            return pidx[name]

        def pcol(name, c, n=1, rows=slice(0, 128)):
            o = pidx[name] + c
            return par[rows, o:o + n]

        rot = {"A": [0, 1, 2, 3], "lo": [0, 1], "mid": [2, 3]}
        rpos = {k: 0 for k in rot}

        def bank(pool):
            b = rot[pool][rpos[pool] % len(rot[pool])]
            rpos[pool] += 1
            return b

        wpos = [0]
        wring = [4]

        def wbuf():
            k = (wpos[0] + 1) % wring[0]
            wpos[0] = k
            return k

        NWT = 48 * depth
        wscr = nc.dram_tensor("wscr", [NWT, 128, 4096], BF16).ap()
        wst = {"seq": 0, "t": 0}

        def load_cols(src2d, nk, col0, ncols, k=None, dcol0=0, width=None, final=True):
            if k is None:
                k = wbuf()
            if width is None:
                width = ncols
            n = nk * width
            wv = WB[k][:, 0:n].rearrange("p (kc c) -> p kc c", c=width)
            step = max(1, nk // 2)
            wid = wst["seq"]
            if wst["t"] == 0:
                src = src2d[:, col0:col0 + ncols].rearrange("(kc p) c -> p kc c", p=128)
                for k0 in range(0, nk, step):
                    k1 = min(nk, k0 + step)
                    S.dma("pool", wv[:, k0:k1, dcol0:dcol0 + ncols], src[:, k0:k1, :],
                          writes=[("W", k, kc) for kc in range(k0, k1)], war=[("Wall", k)])
                if final:
                    S.dma("sp", wscr[wid][:, 0:n], WB[k][:, 0:n], reads=[("W", k, kc) for kc in range(nk)] + [("Wall", k)],
                          writes=[("wscr", wid)])
            elif final:
                for k0 in range(0, nk, step):
                    k1 = min(nk, k0 + step)
                    S.dma("pool", WB[k][:, k0 * width:k1 * width], wscr[wid][:, k0 * width:k1 * width], reads=[("wscr", wid)],
                          writes=[("W", k, kc) for kc in range(k0, k1)], war=[("Wall", k)])
            if final:
                wst["seq"] += 1
                assert wst["seq"] <= NWT
            return k, wv

        def proj_fm(k, wv, c0, m, pool="A"):
            b = bank(pool)
            for kc in range(8):
                MM(P[b][0:m, :], wv[:, kc, c0:c0 + m], xn[:, kc, :], kc == 0, kc == 7,
                   [("W", k, kc), ("Wall", k), ("xn", kc)], PK(b))
            return b

        def proj_tm(k, wv, c0, n, blk, pool="A"):
            b = bank(pool)
            for kc in range(8):
                MM(P[b][:, 0:n], xn[:, kc, blk * 128:(blk + 1) * 128], wv[:, kc, c0:c0 + n], kc == 0, kc == 7,
                   [("W", k, kc), ("Wall", k), ("xn", kc)], PK(b))
            return b

        S.dma("sp", par[:], par_d, writes=["par"])
        S.dma("sp", ggw[:], ggw_d, writes=["ggw"])
        S.dma("sp", cst[:], cst_d, writes=["cst"])
        S.dma("sp", selc[:], sel_d[:, 1024:1536], writes=["sel"])
        S.dma("pool", bd[:], bd_d, writes=["bd"])
        S.dma("pool", idb[:], cst_d[:, CST_ID:CST_ID + 128], writes=["idb"])
        S.dma("pool", mnb[:], cst_d[:, CST_MN:CST_MN + 128], writes=["mnb"])
        S.dma("pool", selb[:], sel_d[:, 0:1024], writes=["selb"])
        MEMSET(ones[:], 1.0, ["ones"])
        MEMSET(hm[:], 0.0, ["hm"])
        MEMSET(hm[0:64, 0:1], 1.0, ["hm"])
        MEMSET(hm[64:128, 1:2], 1.0, ["hm"])
        for nm, tns in (("HSf", HSf), ("HSb", HSb), ("GSf", GSf), ("GSb", GSb), ("SSf", SSf), ("SSb", SSb),
                        ("stail", stail), ("ltail", ltail), ("lcar", lcar), ("lbt", lbt)):
            MEMSET(tns[:], 0.0, [nm])
        if depth == 2:
            TT(lbt[:, 1, :], pcol("hlb1", 0, 4), pcol("hlb0", 0, 4), ALU.subtract, ["par", "lbt"], ["lbt"])
            ACT(lbt[:, 1, :], lbt[:, 1, :], AF.Sigmoid, ["lbt"], ["lbt"])
        TS(omlb[:], lbt[:], -1.0, 1.0, ALU.mult, ALU.add, ["lbt"], ["omlb"])
        for l in range(depth):
            ACT(aneg[:, l:l + 1], pcol(f"salog{l}", 0, 1, slice(0, 8)), AF.Exp, ["par"], ["aneg"])
            TS(aneg[:, l:l + 1], aneg[:, l:l + 1], -1.0, None, ALU.mult, None, ["aneg"], ["aneg"])
            TS(nggb[:, l, :], pcol(f"ggb{l}", 0, 2), -1.0, None, ALU.mult, None, ["par"], ["nggb"])
            ACT(lc1[:, l, :], pcol(f"llam{l}", 0, 4), AF.Exp, ["par"], ["lc1"], scale=-1.0)
            ACT(lc1[:, l, :], lc1[:, l, :], AF.Ln, ["lc1"], ["lc1"], bias=1.0)
            TS(lc2[:, l, :], lc1[:, l, :], -16.0, None, ALU.mult, None, ["lc1"], ["lc2"])
            TS(lc1[:, l, :], lc1[:, l, :], -8.0, None, ALU.mult, None, ["lc1", "lc2"], ["lc1"])
        CK = ["par", "cst", "sel", "ggw", "bd", "idb", "ones", "lbt", "omlb", "aneg", "nggb", "lc1", "lc2"]

        def dump(name, ap, shape, keys):
            if name in dbg_names and name not in dbg_d:
                d = nc.dram_tensor("dbg_" + name, list(shape), F32, kind="ExternalOutput").ap()
                dbg_d[name] = d
                S.dma("pool", d, ap, reads=keys, writes=[("dbg", name)])

        sq_state = {"pending": None, "cnt": 0}

        def sq_feed(m):
            s = sq_state["cnt"] % 2
            prev = sq_state["pending"]
            if prev is not None:
                pm, ps_ = prev
                MM(P[7][:, :], ones[:], sqt[:, ps_, :], pm == 0, False, [("sqt", ps_), "ones"], PK(7))
            ACT(sqt[:, s, :], h[:, m, :], AF.Square, [("h", m)], [("sqt", s)])
            sq_state["pending"] = (sq_state["cnt"], s)
            sq_state["cnt"] += 1

        def sq_flush():
            pm, ps_ = sq_state["pending"]
            MM(P[7][:, :], ones[:], sqt[:, ps_, :], pm == 0, True, [("sqt", ps_), "ones"], PK(7))
            sq_state["pending"], sq_state["cnt"] = None, 0

        def rmsnorm(wname, dst_fn, dkeys_fn, pre=False):
            if pre:
                sq_flush()
                b = 7
            else:
                b = bank("A")
                for m in range(8):
                    s = m % 2
                    ACT(sqt[:, s, :], h[:, m, :], AF.Square, [("h", m)], [("sqt", s)])
                    MM(P[b][:, :], ones[:], sqt[:, s, :], m == 0, m == 7, [("sqt", s), "ones"], PK(b))
            ACT(rs4[:, 0, :], P[b][:, :], AF.Ln, PK(b), [("rs4", 0)], bias=EPS, scale=1.0 / D)
            ACT(rs4[:, 0, :], rs4[:, 0, :], AF.Exp, [("rs4", 0)], [("rs4", 0)], scale=-0.5)
            for m in range(8):
                STT(dst_fn(m), h[:, m, :], pcol(wname, m), rs4[:, 0, :], ALU.mult, ALU.mult,
                    [("h", m), ("rs4", 0), "par"], dkeys_fn(m))

        UFv = FFflat[:, 0:8 * T].rearrange("p (h c v) -> p h c v", h=4, c=8)
        SNv = FFflat[:, 12 * T:16 * T].bitcast(BF16).rearrange("p (h c v) -> p h c v", h=4, c=8)

        def gla_core(l, dk, Qi, Ki, Ei, Sf, Sb, skey, nw_name, ysbase, stage=99, mid_work=None, late_work=None):
            def hp(hd):
                fc = hd if dk == 128 else hd // 2
                pr = slice(0, 128) if dk == 128 else slice((hd % 2) * 64, (hd % 2) * 64 + 64)
                return fc, pr
            if dk == 64:
                MEMSET(FF[:, 12:16, :], 0.0, kFF(12, 13, 14, 15))
            for blk in range(4):
                for cc in range(2):
                    c = 2 * blk + cc
                    rows = slice(cc * 64, cc * 64 + 64)
                    bU = bank("mid")
                    for hd in range(4):
                        fc, pr = hp(hd)
                        if dk == 128:
                            kt = BB[rows, 16 + blk, hd * 128:(hd + 1) * 128]
                            ktk = kBB(16 + blk)
                        else:
                            kt = BB[rows, 16 + blk // 2, (blk % 2) * 256 + hd * 64:(blk % 2) * 256 + hd * 64 + 64]
                            ktk = kBB(16 + blk // 2)
                        MM(P[bU][pr, hd * 128:(hd + 1) * 128], kt, BB[rows, 12 + blk, hd * 128:(hd + 1) * 128],
                           True, True, ktk + kBB(12 + blk), PK(bU))
                    if dk == 128:
                        ACT(UFv[:, :, c, :], P[bU][:, :].rearrange("p (h v) -> p h v", v=128), AF.Copy, PK(bU), kFF(*range(8)))
                    else:
                        for q in range(2):
                            prq = slice(q * 64, q * 64 + 64)
                            ACT(UFv[prq, q::2, c, :], P[bU][prq, :].rearrange("p (h v) -> p h v", v=128)[:, q::2, :], AF.Copy,
                                PK(bU), kFF(*range(8)))
            for blk in range(4):
                bc = slice(blk * 128, (blk + 1) * 128)
                bS = bank("lo")
                for hd in range(4):
                    fc, pr = hp(hd)
                    if dk == 128:
                        MM(P[bS][:, hd * 128:(hd + 1) * 128], BB[:, Ki + fc, bc], BB[:, Qi + fc, bc], True, True,
                           kBB(Ki + fc, Qi + fc), PK(bS))
                    else:
                        MM(P[bS][:, hd * 128:(hd + 1) * 128], BB[:, Ki + fc, bc], BB[:, 6 + hd, bc], True, True,
                           kBB(Ki + fc, 6 + hd), PK(bS))
                TT(STb[:, blk, :, :], P[bS][:, :].rearrange("p (a b) -> p a b", b=128), mask2x4, ALU.mult,
                   PK(bS) + ["cst"], [("ST", blk)])
            for c in range(8):
                for hd in range(4):
                    fc, pr = hp(hd)
                    sfv = Sf[:, l, hd, :] if dk == 128 else Sf[pr, l, fc, :]
                    prev = sfv if c == 0 else UFv[pr, hd, c - 1, :]
                    STT(UFv[pr, hd, c, :], prev, FF[pr, Ei + fc, c * 64 + 63:c * 64 + 64], UFv[pr, hd, c, :], ALU.mult, ALU.add,
                        [(skey, l, hd, "f")] + kFF(2 * hd, 2 * hd + 1, Ei + fc), kFF(2 * hd, 2 * hd + 1))
            if mid_work is not None:
                mid_work()
            for hd in range(4):
                fc, pr = hp(hd)
                sfv = Sf[:, l, hd, :] if dk == 128 else Sf[pr, l, fc, :]
                ACT(SNv[pr, hd, 1:8, :], UFv[pr, hd, 0:7, :], AF.Copy, kFF(2 * hd, 2 * hd + 1), kFF(12 + hd))
                CP(sfv, UFv[pr, hd, 7, :], kFF(2 * hd, 2 * hd + 1), [(skey, l, hd, "f")])
            if late_work is not None:
                late_work()
            for blk in range(4):
                for cc in range(2):
                    c = 2 * blk + cc
                    rows = slice(cc * 64, cc * 64 + 64)
                    ccols = slice(c * 64, c * 64 + 64)
                    for hd in range(4):
                        fc, pr = hp(hd)
                        MM(P[4 + hd][:, ccols], BB[rows, 12 + blk, hd * 128:(hd + 1) * 128],
                           STb[rows, blk, hd, cc * 64:cc * 64 + 64], True, False,
                           kBB(12 + blk) + [("ST", blk)], PK(4 + hd))
                        qi = Qi + fc if dk == 128 else 6 + hd
                        if c == 0:
                            MM(P[4 + hd][:, ccols], Sb[:, l, hd, :], BB[:, qi, ccols], False, True,
                               [(skey, l, hd, "b")] + kBB(qi), PK(4 + hd))
                        else:
                            MM(P[4 + hd][:, ccols], SNv[:, hd, c, :], BB[:, qi, ccols], False, True,
                               kFF(12 + hd) + kBB(qi), PK(4 + hd))
            for hd in range(4):
                fc, pr = hp(hd)
                ACT(Sb[pr, l, hd, :], UFv[pr, hd, 7, :], AF.Copy, kFF(2 * hd, 2 * hd + 1), [(skey, l, hd, "b")])
            nb = [bank("lo"), bank("lo"), bank("mid"), bank("mid")]
            for hd in range(4):
                s = hd % 2
                ACT(sqt[:, s, :], P[4 + hd][:, :], AF.Square, PK(4 + hd), [("sqt", s)])
                MM(P[nb[hd]][:, :], ones[:], sqt[:, s, :], True, True, [("sqt", s), "ones"], PK(nb[hd]))
            for hd in range(4):
                ACT(rs4[:, hd, :], P[nb[hd]][:, :], AF.Ln, PK(nb[hd]), [("rs4", hd)], bias=EPS, scale=1.0 / 128)
            for hd in range(4):
                ACT(rs4[:, hd, :], rs4[:, hd, :], AF.Exp, [("rs4", hd)], [("rs4", hd)], scale=-0.5)
            for hd in range(4):
                s = hd % 2
                TT(tmpf[:, s, :], P[4 + hd][:, :], rs4[:, hd, :], ALU.mult, PK(4 + hd) + [("rs4", hd)], [("tmpf", s)])
                STT(ys[:, ysbase + hd, :], tmpf[:, s, :], pcol(nw_name, 0), G[:, hd, :], ALU.mult, ALU.mult,
                    [("tmpf", s), ("G", hd), "par"], [("ys", ysbase + hd)])

        def vtok_and_ktok(l, col_v, Kdi, dk):
            k, wv = load_cols(w_in_d[l], 8, col_v, 512)
            for blk in range(4):
                b = proj_tm(k, wv, 0, 512, blk)
                ACT(BB[:, 12 + blk, :], P[b][:, :], AF.Copy, PK(b), kBB(12 + blk))
            nfc = 4 if dk == 128 else 2
            for blk in range(4):
                b = bank("A")
                for fc in range(nfc):
                    TR(Pbf[b][:, fc * 128:(fc + 1) * 128], BB[:, Kdi + fc, blk * 128:(blk + 1) * 128], idb[:],
                       kBB(Kdi + fc) + ["idb"], PK(b))
                if dk == 128:
                    CP(BB[:, 16 + blk, :], Pbf[b][:, 0:512], PK(b), kBB(16 + blk))
                else:
                    CP(BB[:, 16 + blk // 2, (blk % 2) * 256:(blk % 2) * 256 + 256], Pbf[b][:, 0:256], PK(b),
                       kBB(16 + blk // 2))

        def conv4(pb, m, tail, l, ci, wname, bname, nci, dst, dkeys, func, tmp=None, tkeys=None, evac_dve=False):
            s = ci % 2
            if tmp is None:
                tmp, tkeys = tmpf[0:m, s, :], [("tmpf", s)]
            if evac_dve:
                CP(XB[0:m, s, 3:3 + T], P[pb][0:m, :], PK(pb), [("XB", s)])
            else:
                ACT(XB[0:m, s, 3:3 + T], P[pb][0:m, :], AF.Copy, PK(pb), [("XB", s)])
            CP(XB[0:m, s, 0:3], tail[0:m, l, ci, :], [("tail", wname, l, ci), ("XB", s)], [("XB", s)])
            TS(tmp, XB[0:m, s, 3:3 + T], pcol(wname, 3 * nci + ci), pcol(bname, ci), ALU.mult, ALU.add,
               [("XB", s), "par"], tkeys)
            for kk in range(3):
                STT(tmp, XB[0:m, s, kk:kk + T], pcol(wname, kk * nci + ci), tmp, ALU.mult, ALU.add,
                    [("XB", s), "par"] + tkeys, tkeys)
            CP(tail[0:m, l, ci, :], XB[0:m, s, T:T + 3], [("XB", s)], [("tail", wname, l, ci)])
            ACT(dst, tmp, func, tkeys, dkeys)

        def layer(l, t):
            rmsnorm(f"nmw{l}", lambda m: xn[:, m, :], lambda m: [("xn", m)], pre=(l > 0))
            dump(f"xn{l}", xn[:], [128, 8, T], [("xn", m) for m in range(8)])
            if stage < 1:
                return
            k, wv = load_cols(w_in_d[l], 8, C_HF, 512)
            for hd in range(4):
                b = proj_fm(k, wv, hd * 128, 128)
                ACT(FF[:, 4 + hd, :], P[b][:, :], AF.Sigmoid, PK(b), kFF(4 + hd))
            if stage < 1.1:
                return
            k, wv = load_cols(w_in_d[l], 8, C_HQ, 512)
            for hd in range(4):
                b = proj_fm(k, wv, hd * 128, 128)
                ACT(FF[:, hd, :], P[b][:, :], AF.Silu, PK(b), kFF(hd))
            for hd in range(4):
                TS(FF[:, 4 + hd, :], FF[:, 4 + hd, :], omlb[:, l, hd:hd + 1], lbt[:, l, hd:hd + 1], ALU.mult, ALU.add,
                   kFF(4 + hd) + ["omlb", "lbt"], kFF(4 + hd))
            for hd in range(4):
                ACT(FF[:, 8 + hd, :], FF[:, 4 + hd, :], AF.Ln, kFF(4 + hd), kFF(8 + hd))
            for hd in range(4):
                TS(FF[:, 4 + hd, :], FF[:, 4 + hd, :], -1.0, 1.0, ALU.mult, ALU.add, kFF(4 + hd), kFF(4 + hd))
                SCAN(FF[:, 12 + hd, :], rmask, FF[:, 8 + hd, :], 0.0, kFF(8 + hd) + ["cst"], kFF(12 + hd))
            for hd in range(4):
                ACT(FF[:, 8 + hd, :], FF[:, 12 + hd, :], AF.Exp, kFF(12 + hd), kFF(8 + hd))
                ACT(FF[:, 16 + hd, :], FF[:, 12 + hd, :], AF.Exp, kFF(12 + hd), kFF(16 + hd), scale=-1.0)
            for hd in range(4):
                TT(BB[:, hd, :], FF[:, hd, :], FF[:, 8 + hd, :], ALU.mult, kFF(hd, 8 + hd), kBB(hd))
                TT(BB[:, 4 + hd, :], FF[:, 4 + hd, :], FF[:, 16 + hd, :], ALU.mult, kFF(4 + hd, 16 + hd), kBB(4 + hd))
                ev = FF[:, 8 + hd, :].rearrange("p (c j) -> p c j", j=64)[:, :, 63:64].to_broadcast([128, 8, 64])
                TT(BB[:, 8 + hd, :].rearrange("p (c j) -> p c j", j=64),
                   BB[:, 4 + hd, :].rearrange("p (c j) -> p c j", j=64), ev, ALU.mult,
                   kBB(4 + hd) + kFF(8 + hd), kBB(8 + hd))
            dump(f"einv{l}", FF[:, 16:20, :], [128, 4, T], kFF(16, 17, 18, 19))
            dump(f"bcum{l}", FF[:, 12:16, :], [128, 4, T], kFF(12, 13, 14, 15))
            dump(f"kt{l}", BB[:, 4:8, :], [128, 4, T], kBB(4, 5, 6, 7))
            dump(f"qt{l}", BB[:, 0:4, :], [128, 4, T], kBB(0, 1, 2, 3))
            if stage < 1.2:
                return
            vtok_and_ktok(l, C_HI, 8, 128)
            if stage < 1.3:
                return
            def hg_work():
                k, wv = load_cols(w_in_d[l], 8, C_HG, 512)
                for hd in range(4):
                    b = proj_fm(k, wv, hd * 128, 128, pool="mid")
                    ACT(G[:, hd, :], P[b][:, :], AF.Silu, PK(b), [("G", hd)])
            def gla_gate():
                k, wv = load_cols(w_in_d[l], 8, C_GLR, 16)
                b = proj_fm(k, wv, 0, 16, pool="lo")
                ACT(tmpf[0:16, 0, :], P[b][0:16, :], AF.Copy, PK(b), [("tmpf", 0)])
                gb = [bank("lo"), bank("lo")]
                for fc in range(2):
                    MM(P[gb[fc]][:, :], ggw[0:16, l * 256 + fc * 128:l * 256 + (fc + 1) * 128], tmpf[0:16, 0, :], True, True,
                       [("tmpf", 0), "ggw"], PK(gb[fc]))
                for fc in range(2):
                    ACT(FF[:, 16 + fc, :], P[gb[fc]][:, :], AF.Exp, PK(gb[fc]) + ["nggb"], kFF(16 + fc), bias=nggb[:, l, fc:fc + 1], scale=-1.0)
                for fc in range(2):
                    ACT(FF[:, 16 + fc, :], FF[:, 16 + fc, :], AF.Ln, kFF(16 + fc), kFF(16 + fc), bias=1.0)
                for fc in range(2):
                    SCAN(FF[:, 18 + fc, :], rmask, FF[:, 16 + fc, :], 0.0, kFF(16 + fc) + ["cst"], kFF(18 + fc))
                for fc in range(2):
                    ACT(FF[:, 8 + fc, :], FF[:, 18 + fc, :], AF.Exp, kFF(18 + fc), kFF(8 + fc), scale=-1.0 / 16)
                    ACT(FF[:, 10 + fc, :], FF[:, 18 + fc, :], AF.Exp, kFF(18 + fc), kFF(10 + fc), scale=1.0 / 16)
            gla_core(l, 128, 0, 4, 8, HSf, HSb, "HS", f"hnw{l}", 0, stage=stage, mid_work=hg_work,
                     late_work=gla_gate if stage >= 2 else None)
            dump(f"ya{l}", ys[:, 0:4, :], [128, 4, T], [("ys", i) for i in range(4)])

            if stage < 2:
                return
            stage_save = stage
            if stage < 2.2:
                return
            k, wv = load_cols(w_in_d[l], 8, C_GQ, 512)
            for fc in range(2):
                b = proj_fm(k, wv, fc * 128, 128)
                STT(BB[:, fc, :], P[b][:, :], 0.125, FF[:, 8 + fc, :], ALU.mult, ALU.mult, PK(b) + kFF(8 + fc), kBB(fc))
            for fc in range(2):
                b = proj_fm(k, wv, 256 + fc * 128, 128)
                TT(BB[:, 2 + fc, :], P[b][:, :], FF[:, 10 + fc, :], ALU.mult, PK(b) + kFF(10 + fc), kBB(2 + fc))
                ev = FF[:, 8 + fc, :].rearrange("p (c j) -> p c j", j=64)[:, :, 63:64].to_broadcast([128, 8, 64])
                TT(BB[:, 4 + fc, :].rearrange("p (c j) -> p c j", j=64),
                   BB[:, 2 + fc, :].rearrange("p (c j) -> p c j", j=64), ev, ALU.mult,
                   kBB(2 + fc) + kFF(8 + fc), kBB(4 + fc))
            if stage < 2.3:
                return
            if stage < 2.4:
                return
            for hd in range(4):
                TS(BB[:, 6 + hd, :], BB[:, hd // 2, :], hm[:, hd % 2:hd % 2 + 1], None, ALU.mult, None,
                   kBB(hd // 2) + ["hm"], kBB(6 + hd))
            vtok_and_ktok(l, C_GV, 4, 64)
            if stage < 2.5:
                return
            def gg_work():
                k, wv = load_cols(w_in_d[l], 8, C_GG, 512)
                for hd in range(4):
                    b = proj_fm(k, wv, hd * 128, 128, pool="mid")
                    ACT(G[:, hd, :], P[b][:, :], AF.Silu, PK(b), [("G", hd)])
            def ssd_prelude():
                k, wv = load_cols(w_in_d[l], 8, C_SX, 512)
                for fc in range(4):
                    b = proj_fm(k, wv, fc * 128, 128, pool="lo")
                    conv4(b, 128, stail, l, fc, f"scw{l}", f"scb{l}", 6, FF[:, 16 + fc, :], kFF(16 + fc), AF.Silu,
                          tmp=FF[:, 10 + fc % 2, :], tkeys=kFF(10 + fc % 2), evac_dve=True)
                k, wv = load_cols(w_in_d[l], 8, C_SB, 256)
                for ci in (4, 5):
                    b = proj_fm(k, wv, (ci - 4) * 128, 128, pool="lo")
                    conv4(b, 128, stail, l, ci, f"scw{l}", f"scb{l}", 6, BB[:, 6 + ci, :], kBB(6 + ci), AF.Silu,
                          tmp=FF[:, 10 + ci % 2, :], tkeys=kFF(10 + ci % 2), evac_dve=True)
            gla_core(l, 64, 0, 2, 8, GSf, GSb, "GS", f"gnw{l}", 8, stage=stage - 1.2, mid_work=gg_work,
                     late_work=ssd_prelude if stage >= 3 else None)
            dump(f"yc{l}", ys[:, 8:12, :], [128, 4, T], [("ys", i) for i in range(8, 12)])

            if stage < 3:
                return
            XBF = [4, 5, 12, 13]
            for fc in range(4):
                ACT(BB[:, XBF[fc], :], FF[:, 16 + fc, :], AF.Copy, kFF(16 + fc), kBB(XBF[fc]))
            k, wv = load_cols(w_in_d[l], 8, C_SDT, 8)
            b = proj_fm(k, wv, 0, 8)
            R8 = slice(0, 8)
            ACT(FF[R8, 5, :], P[b][R8, :], AF.Exp, PK(b) + ["par"], kFF(5), bias=pcol(f"sdtb{l}", 0, 1, R8))
            ACT(FF[R8, 5, :], FF[R8, 5, :], AF.Ln, kFF(5), kFF(5), bias=1.0)
            TS(FF[R8, 6, :], FF[R8, 5, :], aneg[:, l:l + 1], None, ALU.mult, None, kFF(5) + ["aneg"], kFF(6))
            SCAN(FF[R8, 7, :], rmask[R8, :], FF[R8, 6, :], 0.0, kFF(6) + ["cst"], kFF(7))
            TS(FF[R8, 6, :], FF[R8, 7, :], -1.0, None, ALU.mult, None, kFF(7), kFF(6))
            AH = FFflat[0:8, 10 * T:11 * T].bitcast(BF16).rearrange("p (a t) -> p a t", t=T)
            NH = FFflat[0:8, 11 * T:12 * T].bitcast(BF16).rearrange("p (a t) -> p a t", t=T)
            ACT(AH[:, 0, :], FF[R8, 7, :], AF.Copy, kFF(7), kFF(10))
            TT(AH[:, 1, :], FF[R8, 7, :], AH[:, 0, :], ALU.subtract, kFF(7, 10), kFF(10))
            TS(NH[:, 0, :], AH[:, 0, :], -1.0, None, ALU.mult, None, kFF(10), kFF(11))
            TS(NH[:, 1, :], AH[:, 1, :], -1.0, None, ALU.mult, None, kFF(10), kFF(11))
            acs3 = FF[R8, 7, :].rearrange("p (c j) -> p c j", j=64)
            TT(FF[R8, 8, :].rearrange("p (c j) -> p c j", j=64), acs3[:, :, 63:64].to_broadcast([8, 8, 64]), acs3, ALU.subtract,
               kFF(7), kFF(8))
            ACT(FF[R8, 8, :], FF[R8, 8, :], AF.Exp, kFF(8), kFF(8))
            ACT(FF[R8, 9, 0:8], FF[R8, 7, :].rearrange("p (c j) -> p c j", j=64)[:, :, 63], AF.Exp, kFF(7), kFF(9))
            kz, wvz = load_cols(w_in_d[l], 8, C_SZ, 512)
            for fc in range(4):
                bz = proj_fm(kz, wvz, fc * 128, 128)
                ACT(G[:, fc, :], P[bz][:, :], AF.Silu, PK(bz), [("G", fc)])
            for hp in range(4):
                b = bank("A")
                MM(P[b][:, 0:8], selp[:, hp, :], FF[R8, 9, 0:8], True, True, kFF(9) + ["sel"], PK(b))
                CP(decs[:, hp, :], P[b][:, 0:8], PK(b), [("decs", hp)])
                b = bank("A")
                MM(P[b][:, :], selp[:, hp, :], FF[R8, 7, :], True, True, kFF(7) + ["sel"], PK(b))
                s = hp % 2
                ACT(tmpf[:, s, :], P[b][:, :], AF.Exp, PK(b), [("tmpf", s)])
                TT(BB[:, 15 + hp, :], BB[:, 11, :], tmpf[:, s, :], ALU.mult, kBB(11) + [("tmpf", s)], kBB(15 + hp))
            for blk in range(4):
                bc = slice(blk * 128, (blk + 1) * 128)
                b = bank("A")
                TR(P[b][:, 0:8], FF[R8, 5, bc], ident[0:8, 0:8], kFF(5) + ["cst"], PK(b))
                TR(P[b][:, 8:16], FF[R8, 8, bc], ident[0:8, 0:8], kFF(8) + ["cst"], PK(b))
                CP(dtk[:, blk, :], P[b][:, 0:16], PK(b), [("dtk", blk)])
                b = bank("A")
                for fc in range(4):
                    TR(Pbf[b][:, fc * 128:(fc + 1) * 128], BB[:, XBF[fc], bc], idb[:], kBB(XBF[fc]) + ["idb"], PK(b))
                TT(BB[:, 6 + blk, :].rearrange("p (h q) -> p h q", q=64), Pbf[b][:, 0:512].rearrange("p (h q) -> p h q", q=64),
                   dtk[:, blk, 0:8].unsqueeze(2).to_broadcast([128, 8, 64]), ALU.mult, PK(b) + [("dtk", blk)], kBB(6 + blk))
                TT(BB[:, blk, :].rearrange("p (h q) -> p h q", q=64), BB[:, 6 + blk, :].rearrange("p (h q) -> p h q", q=64),
                   dtk[:, blk, 8:16].unsqueeze(2).to_broadcast([128, 8, 64]), ALU.mult, kBB(6 + blk) + [("dtk", blk)], kBB(blk))
                b = bank("A")
                TR(Pbf[b][:, 0:128], BB[:, 10, bc], idb[:], kBB(10) + ["idb"], PK(b))
                CP(BB[:, 14, bc], Pbf[b][:, 0:128], PK(b), kBB(14))
            rpos["A"] = 2
            for g in range(2):
                gr = slice(g * 64, g * 64 + 64)
                for blk in range(4):
                    bc = slice(blk * 128, (blk + 1) * 128)
                    MM(P[g][:, bc], BB[gr, 10, bc], BB[gr, 11, bc], True, True, kBB(10, 11), PK(g))
            UFs = FFflat[:, 12 * T:16 * T].rearrange("p (c h q) -> p c h q", c=8, h=4)
            SNs = FFflat[:, 0:2 * T].bitcast(BF16).rearrange("p (c h q) -> p c h q", c=8, h=4)
            for blk in range(4):
                for cc in range(2):
                    c = 2 * blk + cc
                    rows = slice(cc * 64, cc * 64 + 64)
                    bU = bank("mid")
                    for hh in range(8):
                        g, hp = hh // 4, hh % 4
                        gr = slice(g * 64, g * 64 + 64)
                        MM(P[bU][gr, hp * 64:(hp + 1) * 64], BB[rows, 14, blk * 128 + g * 64:blk * 128 + g * 64 + 64],
                           BB[rows, blk, hh * 64:(hh + 1) * 64], True, True, kBB(14, blk), PK(bU))
                    ACT(UFs[:, c, :, :], P[bU][:, 0:256].rearrange("p (h q) -> p h q", q=64), AF.Copy, PK(bU), kFF(12, 13, 14, 15))
            for c in range(8):
                for hp in range(4):
                    prev = SSf[:, l, hp, :] if c == 0 else UFs[:, c - 1, hp, :]
                    STT(UFs[:, c, hp, :], prev, decs[:, hp, c:c + 1], UFs[:, c, hp, :], ALU.mult, ALU.add,
                        [("SSf", l), ("decs", hp)] + kFF(12, 13, 14, 15), kFF(12, 13, 14, 15))
            ACT(SNs[:, 1:8, :, :], UFs[:, 0:7, :, :], AF.Copy, kFF(12, 13, 14, 15), kFF(0, 1))
            CP(SSf[:, l, :, :], UFs[:, 7, :, :], kFF(12, 13, 14, 15), [("SSf", l)])

            def ssd_mt(blk):
                bc = slice(blk * 128, (blk + 1) * 128)
                s = blk % 2
                for g in range(2):
                    bL = bank("mid")
                    for q in range(4):
                        hh = 4 * g + q
                        qs_ = slice(q * 128, (q + 1) * 128)
                        sb_h = selb[:, hh * 128:(hh + 1) * 128]
                        MM(P[bL][:, qs_], sb_h, AH[:, 0, bc], True, False, kFF(10) + ["selb"], PK(bL))
                        MM(P[bL][:, qs_], sb_h, AH[:, 1, bc], False, False, kFF(10) + ["selb"], PK(bL))
                        MM(P[bL][:, qs_], NH[:, 0, bc], sb_h, False, False, kFF(11) + ["selb"], PK(bL))
                        MM(P[bL][:, qs_], NH[:, 1, bc], sb_h, False, False, kFF(11) + ["selb"], PK(bL))
                        MM(P[bL][:, qs_], idb[:], mnb[:], False, True, ["idb", "mnb"], PK(bL))
                    ACT(LT[:, g, :], P[bL][:, :], AF.Exp, PK(bL), [("LT", g)])
                    TT(MTb[:, s, 4 * g:4 * g + 4, :], LT[:, g, :].rearrange("p (q i) -> p q i", i=128),
                       P[g][:, bc].unsqueeze(1).to_broadcast([128, 4, 128]), ALU.mult,
                       [("P", g), ("LT", g)], [("MT", s, hh_) for hh_ in range(4 * g, 4 * g + 4)])

            def ssd_out(blk):
                s = blk % 2
                for cc in range(2):
                    c = 2 * blk + cc
                    ccols = slice(c * 64, c * 64 + 64)
                    for hh in range(8):
                        g, hp, fc = hh // 4, hh % 4, hh // 2
                        gr = slice(g * 64, g * 64 + 64)
                        pr = slice((hh % 2) * 64, (hh % 2) * 64 + 64)
                        MM(P[4 + fc][pr, ccols], BB[:, 6 + blk, hh * 64:(hh + 1) * 64], MTb[:, s, hh, cc * 64:cc * 64 + 64],
                           True, False, kBB(6 + blk) + [("MT", s, hh)], PK(4 + fc))
                        if c == 0:
                            MM(P[4 + fc][pr, ccols], SSb[gr, l, hp, :], BB[gr, 15 + hp, ccols], False, True,
                               [("SSb", l)] + kBB(15 + hp), PK(4 + fc))
                        else:
                            MM(P[4 + fc][pr, ccols], SNs[gr, c, hp, :], BB[gr, 15 + hp, ccols], False, True,
                               kFF(0, 1) + kBB(15 + hp), PK(4 + fc))

            ssd_mt(0)
            for blk in range(4):
                if blk + 1 < 4:
                    ssd_mt(blk + 1)
                ssd_out(blk)
            ACT(SSb[:, l, :, :], UFs[:, 7, :, :], AF.Copy, kFF(12, 13, 14, 15), [("SSb", l)])
            for fc in range(4):
                s = fc % 2
                STT(tmpf[:, s, :], FF[:, 16 + fc, :], pcol(f"sd{l}", fc), P[4 + fc][:, :], ALU.mult, ALU.add,
                    kFF(16 + fc) + PK(4 + fc) + ["par"], [("tmpf", s)])
                TT(FF[:, 8 + fc, :], tmpf[:, s, :], G[:, fc, :], ALU.mult, [("tmpf", s), ("G", fc)], kFF(8 + fc))
            for g in range(2):
                b = bank("mid")
                for q in range(2):
                    fc = 2 * g + q
                    ACT(sqt[:, q, :], FF[:, 8 + fc, :], AF.Square, kFF(8 + fc), [("sqt", q)])
                    MM(P[b][:, :], ones[:], sqt[:, q, :], q == 0, q == 1, [("sqt", q), "ones"], PK(b))
                ACT(rs4[:, g, :], P[b][:, :], AF.Ln, PK(b), [("rs4", g)], bias=EPS, scale=1.0 / 256)
            for g in range(2):
                ACT(rs4[:, g, :], rs4[:, g, :], AF.Exp, [("rs4", g)], [("rs4", g)], scale=-0.5)
            for g in range(2):
                for q in range(2):
                    fc = 2 * g + q
                    STT(ys[:, 4 + fc, :], FF[:, 8 + fc, :], pcol(f"snw{l}", fc), rs4[:, g, :], ALU.mult, ALU.mult,
                        kFF(8 + fc) + [("rs4", g), "par"], [("ys", 4 + fc)])
            dump(f"yb{l}", ys[:, 4:8, :], [128, 4, T], [("ys", i) for i in range(4, 8)])

            if stage < 4:
                return
            lru_w = {}

            def lru_unit_a(fc):
                if fc == 0:
                    lru_w["x"] = load_cols(w_in_d[l], 8, C_LX, 512, k=3)
                k, wv = lru_w["x"]
                b = proj_fm(k, wv, fc * 128, 128)
                gs = fc % 2
                conv4(b, 128, ltail, l, fc, f"lcw{l}", f"lcb{l}", 4, FF[:, 8 + fc, :], kFF(8 + fc), AF.Copy,
                      tmp=G[:, gs, :], tkeys=[("G", gs)])
                ACT(BB[:, fc, :], FF[:, 8 + fc, :], AF.Copy, kFF(8 + fc), kBB(fc))

            def lru_unit_a2(fc):
                g2 = 2 + fc % 2
                b = bank("A")
                MM(P[b][:, :], bd[:, (l * 2 + 0) * 512 + fc * 128:(l * 2 + 0) * 512 + (fc + 1) * 128], BB[:, fc, :], True, True,
                   kBB(fc) + ["bd"], PK(b))
                ACT(FF[:, 12 + fc, :], P[b][:, :], AF.Sigmoid, PK(b) + ["par"], kFF(12 + fc), bias=pcol(f"lba{l}", fc))
                b = bank("A")
                MM(P[b][:, :], bd[:, (l * 2 + 1) * 512 + fc * 128:(l * 2 + 1) * 512 + (fc + 1) * 128], BB[:, fc, :], True, True,
                   kBB(fc) + ["bd"], PK(b))
                ACT(FF[:, 16 + fc, :], P[b][:, :], AF.Sigmoid, PK(b) + ["par"], kFF(16 + fc), bias=pcol(f"lbx{l}", fc))
                ACT(G[:, g2, :], FF[:, 12 + fc, :], AF.Exp, kFF(12 + fc) + ["lc2"], [("G", g2)], scale=lc2[:, l, fc:fc + 1])
                ACT(FF[:, 12 + fc, :], FF[:, 12 + fc, :], AF.Exp, kFF(12 + fc) + ["lc1"], kFF(12 + fc), scale=lc1[:, l, fc:fc + 1])
                ACT(G[:, g2, :], G[:, g2, :], AF.Sqrt, [("G", g2)], [("G", g2)], bias=1.0, scale=-1.0)
                TT(FF[:, 16 + fc, :], FF[:, 16 + fc, :], FF[:, 8 + fc, :], ALU.mult, kFF(16 + fc, 8 + fc), kFF(16 + fc))
                TT(FF[:, 16 + fc, :], FF[:, 16 + fc, :], G[:, g2, :], ALU.mult, kFF(16 + fc) + [("G", g2)], kFF(16 + fc))
                SCAN(FF[:, 8 + fc, :], FF[:, 12 + fc, :], FF[:, 16 + fc, :], lcar[:, l, fc:fc + 1],
                     kFF(12 + fc, 16 + fc) + [("lcar", l, fc)], kFF(8 + fc))
                CP(lcar[:, l, fc:fc + 1], FF[:, 8 + fc, T - 1:T], kFF(8 + fc), [("lcar", l, fc)])

            def lru_unit_b(fc):
                if fc == 0:
                    lru_w["g"] = load_cols(w_in_d[l], 8, C_LG, 512, k=3)
                k, wv = lru_w["g"]
                b = proj_fm(k, wv, fc * 128, 128)
                tg, tk = FF[:, 12 + fc, :], kFF(12 + fc)
                ACT(tg, P[b][:, :], AF.Square, PK(b), tk)
                TS(tg, tg, 0.044715, 1.0, ALU.mult, ALU.add, tk, tk)
                TT(tg, tg, P[b][:, :], ALU.mult, tk + PK(b), tk)
                ACT(tg, tg, AF.Sigmoid, tk, tk, scale=1.5957691216057308)
                TT(tg, tg, P[b][:, :], ALU.mult, tk + PK(b), tk)
                TT(ys[:, 12 + fc, :], tg, FF[:, 8 + fc, :], ALU.mult, tk + kFF(8 + fc), [("ys", 12 + fc)])

            lru_units = []
            for fc in range(4):
                lru_units += [(lru_unit_a, fc), (lru_unit_a2, fc)]
            lru_units += [(lru_unit_b, fc) for fc in range(4)]
            if stage < 4.5:
                lru_units = []

            if lru_units:
                wring[0] = 3
            for n in range(4):
                kb, wvb = load_cols(w_br_d[l, n], 4, 0, 1024)
                for half in range(2):
                    kg, wvg = load_cols(w_in_d[l], 8, C_MG + n * 1024 + half * 512, 512)
                    for mm_ in range(4):
                        m = half * 4 + mm_
                        pbk = bank("A")
                        for kc in range(4):
                            MM(P[pbk][:, :], wvb[:, kc, m * 128:(m + 1) * 128], ys[:, n * 4 + kc, :], kc == 0, kc == 3,
                               [("W", kb, kc), ("Wall", kb), ("ys", n * 4 + kc)], PK(pbk))
                        pg = proj_fm(kg, wvg, mm_ * 128, 128)
                        s = m % 2
                        ACT(tmpf[:, s, :], P[pg][:, :], AF.Sigmoid, PK(pg), [("tmpf", s)])
                        if n == 0:
                            TT(FF[:, m, :], tmpf[:, s, :], P[pbk][:, :], ALU.mult, [("tmpf", s)] + PK(pbk), kFF(m))
                        else:
                            TT(tmpf[:, s, :], tmpf[:, s, :], P[pbk][:, :], ALU.mult, [("tmpf", s)] + PK(pbk), [("tmpf", s)])
                            if n < 3:
                                TT(FF[:, m, :], FF[:, m, :], tmpf[:, s, :], ALU.add, kFF(m) + [("tmpf", s)], kFF(m))
                            else:
                                TT(BB[:, 4 + m, :], FF[:, m, :], tmpf[:, s, :], ALU.add, kFF(m) + [("tmpf", s)], kBB(4 + m))
                        if n < 3 and (n * 8 + m) % 2 == 1 and lru_units:
                            fn_, fc_ = lru_units.pop(0)
                            fn_(fc_)
                            if not lru_units:
                                wring[0] = 4
            for half in range(2):
                ko, wvo = load_cols(w_out_d[l], 8, half * 512, 512)
                for mm_ in range(4):
                    m = half * 4 + mm_
                    b = bank("A")
                    for kc in range(8):
                        MM(P[b][:, :], wvo[:, kc, mm_ * 128:(mm_ + 1) * 128], BB[:, 4 + kc, :], kc == 0, kc == 7,
                           [("W", ko, kc), ("Wall", ko)] + kBB(4 + kc), PK(b))
                    TT(h[:, m, :], h[:, m, :], P[b][:, :], ALU.add, [("h", m)] + PK(b), [("h", m)])
                    sq_feed(m)
            dump(f"hmix{l}", h[:], [128, 8, T], [("h", m) for m in range(8)])

            if stage < 6:
                return
            rmsnorm(f"nfw{l}", lambda m: xn[:, m, :], lambda m: [("xn", m)], pre=True)
            for jg in range(11):
                kf = wbuf()
                load_cols(w_f1_d[l], 8, jg * 256, 256, k=kf, dcol0=0, width=512, final=False)
                _, wvf = load_cols(w_f1_d[l], 8, D_FF + jg * 256, 256, k=kf, dcol0=256, width=512)
                for jj in range(2):
                    j = 2 * jg + jj
                    pg = proj_fm(kf, wvf, jj * 128, 128)
                    pu = proj_fm(kf, wvf, 256 + jj * 128, 128)
                    s = j % 2
                    ACT(tmpf[:, s, :], P[pg][:, :], AF.Silu, PK(pg), [("tmpf", s)])
                    TT(HID[:, j, :], tmpf[:, s, :], P[pu][:, :], ALU.mult, [("tmpf", s)] + PK(pu), kHID(j))
            for cg in range(4):
                k2 = []
                for kh in range(2):
                    kk_, wv2 = load_cols(w_f2_d[l][kh * 1408:(kh + 1) * 1408, :], 11, cg * 256, 256)
                    k2.append((kk_, wv2))
                for mm_ in range(2):
                    m = cg * 2 + mm_
                    b = bank("A")
                    for j in range(22):
                        kk_, wv2 = k2[j // 11]
                        MM(P[b][:, :], wv2[:, j % 11, mm_ * 128:(mm_ + 1) * 128], HID[:, j, :], j == 0, j == 21,
                           [("W", kk_, j % 11), ("Wall", kk_)] + kHID(j), PK(b))
                    TT(h[:, m, :], h[:, m, :], P[b][:, :], ALU.add, [("h", m)] + PK(b), [("h", m)])
                    sq_feed(m)
            dump(f"hout{l}", h[:], [128, 8, T], [("h", m) for m in range(8)])

        for t in range(nt):
            t0 = t * T
            wst["seq"], wst["t"] = 0, t
            if t == 0:
                S.dma("sp", XIN[:], x_d[0:128, :], writes=["XIN"])
            for blk in range(4):
                s = blk % 2
                if blk == 0:
                    xsrc, xk = XIN, ["XIN"]
                else:
                    xsrc, xk = IO[s], kIO(s)
                    S.dma("sp", IO[s], x_d[t0 + blk * 128:t0 + (blk + 1) * 128, :], writes=kIO(s))
                for hf in range(2):
                    b = bank("A")
                    for q in range(4):
                        m = hf * 4 + q
                        TR(P[b][:, q * 128:(q + 1) * 128], xsrc[:, m * 128:(m + 1) * 128], ident, xk + ["cst"], PK(b))
                    ACT(h[:, hf * 4:(hf + 1) * 4, blk * 128:(blk + 1) * 128], P[b][:, :].rearrange("p (a b) -> p a b", b=128),
                        AF.Copy, PK(b), [("h", hf * 4 + q) for q in range(4)])
            if t + 1 < nt:
                S.dma("sp", XIN[:], x_d[t0 + T:t0 + T + 128, :], writes=["XIN"])
            for l in range(depth):
                if stage >= 0:
                    layer(l, t)
            rmsnorm("nf", lambda m: FF[:, m, :], lambda m: kFF(m), pre=True)
            for blk in range(4):
                s = blk % 2
                for hf in range(2):
                    b = bank("A")
                    for q in range(4):
                        m = hf * 4 + q
                        TR(P[b][:, q * 128:(q + 1) * 128], FF[:, m, blk * 128:(blk + 1) * 128], ident, kFF(m) + ["cst"], PK(b))
                    ACT(IO[s][:, hf * 512:(hf + 1) * 512], P[b][:, :], AF.Copy, PK(b), kIO(s))
                S.dma("sp", out_d[t0 + blk * 128:t0 + (blk + 1) * 128, :], IO[s], reads=kIO(s), writes=[("out", t, blk)])
        S.finish("sp", [("out", t, blk) for t in range(nt) for blk in range(4)] + [("dbg", n) for n in dbg_d])
        S.replay()
    return nc, S


def prep_shared(inputs, depth):
    pk = pack_params(inputs, depth)
    cst, selc = make_consts()
    ggw = np.concatenate([np.asarray(inputs["gla_gate_w"][l], np.float32) for l in range(depth)], axis=1)
    bd = np.concatenate([np.concatenate([_blockdiag(np.asarray(inputs["lru_wa"][l], np.float32)),
                                         _blockdiag(np.asarray(inputs["lru_wx"][l], np.float32))], axis=1)
                         for l in range(depth)], axis=1)
    shared = {
        "w_in": np.ascontiguousarray(inputs["w_in"][:depth], dtype=np.float32),
        "w_branch": np.ascontiguousarray(inputs["w_branch"][:depth], dtype=np.float32),
        "w_out": np.ascontiguousarray(inputs["w_out"][:depth], dtype=np.float32),
        "w_ffn_in": np.ascontiguousarray(inputs["w_ffn_in"][:depth], dtype=np.float32),
        "w_ffn_out": np.ascontiguousarray(inputs["w_ffn_out"][:depth], dtype=np.float32),
        "params": pk.arr(), "ggw": np.ascontiguousarray(ggw), "bd": np.ascontiguousarray(bd),
        "cst": cst, "selc": selc,
    }
    return pk, shared


def kernel(**inputs):
    x = np.asarray(inputs["x"], np.float32)
    B, L, _ = x.shape
    pk, shared = prep_shared(inputs, DEPTH)
    nc, _ = build(L // T, DEPTH, pk.idx, pk.off)
    in_maps = [dict(shared, x=np.ascontiguousarray(x[b])) for b in range(B)]
    res = run_bass_kernel_spmd(nc, in_maps, core_ids=list(range(B)))
    return np.stack([r["out"] for r in res.results], axis=0).astype(np.float32)
```
